# Optimizing a Trainium2 kernel written in Bass

```python
import math
import jax, jax.numpy as jnp
from jax import lax
import numpy as np

D_MODEL = 1024
BATCH = 4
SEQ = 4096
DEPTH = 2

D_MIX = 1024
W_GRP = D_MIX // 4
HEAD_DIM = 64
N_HEADS = W_GRP // HEAD_DIM
NSA_DK = 64
CMP_STRIDE = 16
CMP_BLOCK = 2 * CMP_STRIDE
CMP_HIDDEN = 128
SEL_BLOCK = 64
N_SEL = 16
WINDOW = 512
Q_BLOCK = 128
REL_BUCKETS = 32
REL_MAX_EXACT = REL_BUCKETS // 2
REL_MAX_DIST = 128
SGU_CHUNK = 128
CONV_WIDTH = 31
RWKV_LORA = 32
RWKV_GN_EPS = 64e-5
RWKV_SHIFT_W = 3 * W_GRP + 2 * RWKV_LORA
PLE_DIM = 256
NORM_EPS = 1e-6
NEG = -1e30
FORCE = 1e4

PROJ_WIDTHS = [
    W_GRP, NSA_DK, NSA_DK, NSA_DK, NSA_DK, NSA_DK, NSA_DK, 3 * N_HEADS, W_GRP,
    W_GRP, W_GRP, W_GRP,
    W_GRP, W_GRP, W_GRP,
    RWKV_SHIFT_W, W_GRP,
]
N_IN = sum(PROJ_WIDTHS)

kernel_name = "hybrid_nsa_gmlp_conformer_rwkv7_block"


def rms_norm(x, g):
    xf = x.astype(jnp.float32)
    return xf * lax.rsqrt(jnp.mean(xf * xf, -1, keepdims=True) + NORM_EPS) * g


def layer_norm(x, g, b, eps=1e-5):
    xf = x.astype(jnp.float32)
    mu = jnp.mean(xf, -1, keepdims=True)
    var = jnp.mean(jnp.square(xf - mu), -1, keepdims=True)
    return (xf - mu) * lax.rsqrt(var + eps) * g + b


def rel_bucket(dist):
    n = jnp.maximum(dist, 0)
    nf = jnp.maximum(n, REL_MAX_EXACT).astype(jnp.float32)
    large = REL_MAX_EXACT + (jnp.log(nf / REL_MAX_EXACT) / math.log(REL_MAX_DIST / REL_MAX_EXACT)
                             * (REL_BUCKETS - REL_MAX_EXACT)).astype(jnp.int32)
    large = jnp.minimum(large, REL_BUCKETS - 1)
    return jnp.where(n < REL_MAX_EXACT, n, large)


def masked_softmax(logits, mask):
    p = jax.nn.softmax(jnp.where(mask, logits.astype(jnp.float32), NEG), axis=-1)
    return jnp.where(mask, p, 0.0)


def nsa_compress(k, pos, w1, w2):
    B, S, dk = k.shape
    ch = k.reshape(B, S // CMP_STRIDE, CMP_STRIDE, dk)
    blk = jnp.concatenate([ch[:, :-1], ch[:, 1:]], axis=2) + pos
    flat = blk.reshape(B, -1, CMP_BLOCK * dk)
    return jax.nn.silu(flat @ w1) @ w2


def nsa_attention(q, kc, vc, k_sel, v_sel, k_win, v_win, gates, rel_bias):
    B, S, H, dk = q.shape
    scale = dk ** -0.5
    n_cmp = kc.shape[1]
    n_blk = S // SEL_BLOCK
    n_top = min(N_SEL, n_blk)
    ratio = SEL_BLOCK // CMP_STRIDE
    cmp_end = jnp.arange(n_cmp) * CMP_STRIDE + CMP_BLOCK - 1
    ks_blk = k_sel.reshape(B, n_blk, SEL_BLOCK, dk)
    vs_blk = v_sel.reshape(B, n_blk, SEL_BLOCK, dk)
    kw_pad = jnp.pad(k_win, ((0, 0), (WINDOW, 0), (0, 0)))
    vw_pad = jnp.pad(v_win, ((0, 0), (WINDOW, 0), (0, 0)))
    gather = jax.vmap(lambda blocks, idx: blocks[idx])
    j = jnp.arange(n_blk)

    def one_block(c):
        t0 = c * Q_BLOCK
        t = t0 + jnp.arange(Q_BLOCK)
        qc = lax.dynamic_slice_in_dim(q, t0, Q_BLOCK, axis=1).astype(jnp.float32)
        gc = jax.nn.sigmoid(lax.dynamic_slice_in_dim(gates, t0, Q_BLOCK, axis=1).astype(jnp.float32))
        d_c = t[:, None] - cmp_end[None, :]
        b_c = jnp.moveaxis(rel_bias[rel_bucket(d_c)], -1, 0)
        s_c = jnp.einsum('bqhd,bnd->bhqn', qc, kc) * scale + b_c
        p_c = masked_softmax(s_c, d_c >= 0)
        o_c = jnp.einsum('bhqn,bnd->bqhd', p_c, vc)
        pc = jnp.pad(p_c.sum(1), ((0, 0), (0, 0), (1, ratio)))
        imp = (pc[..., :n_blk * ratio].reshape(B, Q_BLOCK, n_blk, ratio).sum(-1)
               + pc[..., ratio::ratio][..., :n_blk])
        cur = t // SEL_BLOCK
        causal_blk = j[None, :] <= cur[:, None]
        forced = (j[None, :] == 0) | (j[None, :] == cur[:, None]) | (j[None, :] == cur[:, None] - 1)
        imp = jnp.where(causal_blk, jnp.where(forced, FORCE, imp), NEG)
        top_val, top_idx = lax.top_k(imp, n_top)
        ksel = gather(ks_blk, top_idx)
        vsel = gather(vs_blk, top_idx)
        kpos = top_idx[..., None] * SEL_BLOCK + jnp.arange(SEL_BLOCK)
        d_s = t[None, :, None, None] - kpos
        m_s = (top_val[..., None] > 0.5 * NEG) & (d_s >= 0)
        b_s = jnp.moveaxis(rel_bias[rel_bucket(d_s)], -1, 1)
        s_s = jnp.einsum('bqhd,bqnkd->bhqnk', qc, ksel) * scale + b_s
        p_s = masked_softmax(s_s.reshape(B, H, Q_BLOCK, -1), m_s.reshape(B, 1, Q_BLOCK, -1))
        o_s = jnp.einsum('bhqm,bqmd->bqhd', p_s, vsel.reshape(B, Q_BLOCK, -1, dk))
        kw = lax.dynamic_slice_in_dim(kw_pad, t0, Q_BLOCK + WINDOW, axis=1)
        vw = lax.dynamic_slice_in_dim(vw_pad, t0, Q_BLOCK + WINDOW, axis=1)
        kpos_w = t0 - WINDOW + jnp.arange(Q_BLOCK + WINDOW)
        d_w = t[:, None] - kpos_w[None, :]
        m_w = (d_w >= 0) & (d_w < WINDOW) & (kpos_w[None, :] >= 0)
        b_w = jnp.moveaxis(rel_bias[rel_bucket(d_w)], -1, 0)
        s_w = jnp.einsum('bqhd,bkd->bhqk', qc, kw) * scale + b_w
        p_w = masked_softmax(s_w, m_w)
        o_w = jnp.einsum('bhqk,bkd->bqhd', p_w, vw)
        o = gc[..., 0:1] * o_c + gc[..., 1:2] * o_s + gc[..., 2:3] * o_w
        return o.reshape(B, Q_BLOCK, H * dk)

    out = lax.map(one_block, jnp.arange(S // Q_BLOCK))
    return jnp.moveaxis(out, 0, 1).reshape(B, S, H * dk)


def spatial_gating(u, v, ln_g, ln_b, w_s, b_s):
    B, S, C = u.shape
    u = jax.nn.gelu(u)
    v = layer_norm(jax.nn.gelu(v), ln_g, ln_b)
    vc = v.reshape(B, S // SGU_CHUNK, SGU_CHUNK, N_HEADS, C // N_HEADS)
    w = w_s * jnp.tril(jnp.ones((SGU_CHUNK, SGU_CHUNK), w_s.dtype))
    sv = jnp.einsum('hts,bcshd->bcthd', w, vc) + b_s.T[None, None, :, :, None]
    return u * sv.reshape(B, S, C)


def conv_module(a, b, w_dw, b_dw, ln_g, ln_b, w_pw, b_pw):
    x = a * jax.nn.sigmoid(b)
    C = x.shape[-1]
    y = lax.conv_general_dilated(x, w_dw[:, None, :].astype(x.dtype), window_strides=(1,),
                                 padding=((CONV_WIDTH - 1, 0),),
                                 dimension_numbers=('NWC', 'WIO', 'NWC'),
                                 feature_group_count=C) + b_dw
    y = jax.nn.silu(layer_norm(y, ln_g, ln_b))
    return y @ w_pw + b_pw


def token_shift(z, mu):
    prev = jnp.pad(z, ((0, 0), (1, 0), (0, 0)))[:, :-1]
    return z + mu * (prev - z)


def rwkv7_time_mix(xs, w0, w_up, a0, a_up, k_k, k_a, r_k, gn_g, gn_b):
    B, S, _ = xs.shape
    C, H, N, L = W_GRP, N_HEADS, HEAD_DIM, RWKV_LORA
    r, k, v = xs[..., :C], xs[..., C:2 * C], xs[..., 2 * C:3 * C]
    wd, ad = xs[..., 3 * C:3 * C + L], xs[..., 3 * C + L:]
    w_log = -jax.nn.softplus(-(w0 + jnp.tanh(wd) @ w_up)) - 0.5
    decay = jnp.exp(-jnp.exp(w_log.astype(jnp.float32)))
    a = jax.nn.sigmoid(a0 + ad @ a_up)
    kk = (k * k_k).astype(jnp.float32).reshape(B, S, H, N)
    kk = kk / jnp.maximum(jnp.sqrt(jnp.sum(kk * kk, -1, keepdims=True)), 1e-12)
    k = k * (1 + (a - 1) * k_a)

    def heads(z):
        return jnp.moveaxis(z.astype(jnp.float32).reshape(B, S, H, N), 1, 0)

    def step(state, inp):
        r_t, w_t, k_t, v_t, kk_t, a_t = inp
        sa = jnp.einsum('bhvk,bhk->bhv', state, -kk_t)
        state = (state * w_t[:, :, None, :]
                 + sa[..., None] * (kk_t * a_t)[:, :, None, :]
                 + v_t[..., None] * k_t[:, :, None, :])
        return state, jnp.einsum('bhvk,bhk->bhv', state, r_t)

    state0 = jnp.zeros((B, H, N, N), jnp.float32)
    _, y = lax.scan(step, state0, (heads(r), heads(decay), heads(k), heads(v), heads(kk), heads(a)))
    y = jnp.moveaxis(y, 0, 1)
    mu = jnp.mean(y, -1, keepdims=True)
    var = jnp.mean(jnp.square(y - mu), -1, keepdims=True)
    yn = ((y - mu) * lax.rsqrt(var + RWKV_GN_EPS)).reshape(B, S, C) * gn_g + gn_b
    rh = r.astype(jnp.float32).reshape(B, S, H, N)
    kh = k.astype(jnp.float32).reshape(B, S, H, N)
    vh = v.astype(jnp.float32).reshape(B, S, H, N)
    bonus = (jnp.sum(rh * kh * r_k, -1, keepdims=True) * vh).reshape(B, S, C)
    return yn + bonus


def setup_inputs(seed: int = 0) -> dict:
    key = jax.random.key(seed)
    ks = iter(jax.random.split(key, 40))
    nrm = lambda shape, s=1.0: s * jax.random.normal(next(ks), shape, jnp.float32)
    D, L, C, H = DEPTH, RWKV_LORA, W_GRP, N_HEADS
    return {
        "x": nrm((BATCH, SEQ, D_MODEL)),
        "p": nrm((DEPTH, BATCH, SEQ, PLE_DIM)),
        "rel_bias": nrm((REL_BUCKETS, N_HEADS), 0.5),
        "w_in": nrm((D, D_MODEL, N_IN), D_MODEL ** -0.5),
        "w_out": nrm((D, D_MIX, D_MODEL), D_MIX ** -0.5),
        "g_pre": 1.0 + nrm((D, D_MODEL), 0.05),
        "g_post": 1.0 + nrm((D, D_MODEL), 0.05),
        "nsa_pos": nrm((D, 2, CMP_BLOCK, NSA_DK), 0.1),
        "nsa_w1": nrm((D, 2, CMP_BLOCK * NSA_DK, CMP_HIDDEN), (CMP_BLOCK * NSA_DK) ** -0.5),
        "nsa_w2": nrm((D, 2, CMP_HIDDEN, NSA_DK), CMP_HIDDEN ** -0.5),
        "sgu_ln_g": 1.0 + nrm((D, C), 0.05),
        "sgu_ln_b": nrm((D, C), 0.02),
        "sgu_w": nrm((D, H, SGU_CHUNK, SGU_CHUNK), SGU_CHUNK ** -0.5),
        "sgu_b": 1.0 + nrm((D, H, SGU_CHUNK), 0.05),
        "conv_w": nrm((D, CONV_WIDTH, C), CONV_WIDTH ** -0.5),
        "conv_b": nrm((D, C), 0.02),
        "conv_ln_g": 1.0 + nrm((D, C), 0.05),
        "conv_ln_b": nrm((D, C), 0.02),
        "conv_pw": nrm((D, C, C), C ** -0.5),
        "conv_pw_b": nrm((D, C), 0.02),
        "rwkv_mu": jax.random.uniform(next(ks), (D, RWKV_SHIFT_W), jnp.float32),
        "rwkv_w0": -2.0 + nrm((D, C), 1.0),
        "rwkv_w_up": nrm((D, L, C), 0.5 * L ** -0.5),
        "rwkv_a0": nrm((D, C), 0.1),
        "rwkv_a_up": nrm((D, L, C), 0.5 * L ** -0.5),
        "rwkv_k_k": 0.85 + nrm((D, C), 0.05),
        "rwkv_k_a": 1.0 + nrm((D, C), 0.05),
        "rwkv_r_k": nrm((D, H, HEAD_DIM), 0.1),
        "rwkv_gn_g": 1.0 + nrm((D, C), 0.05),
        "rwkv_gn_b": nrm((D, C), 0.02),
        "ple_proj": nrm((D, PLE_DIM, D_MODEL), PLE_DIM ** -0.5),
        "ple_gate": nrm((D, D_MODEL, D_MODEL), D_MODEL ** -0.5),
    }


def reference(x, p, rel_bias, w_in, w_out, g_pre, g_post, nsa_pos, nsa_w1, nsa_w2,
              sgu_ln_g, sgu_ln_b, sgu_w, sgu_b, conv_w, conv_b, conv_ln_g, conv_ln_b,
              conv_pw, conv_pw_b, rwkv_mu, rwkv_w0, rwkv_w_up, rwkv_a0, rwkv_a_up,
              rwkv_k_k, rwkv_k_a, rwkv_r_k, rwkv_gn_g, rwkv_gn_b, ple_proj, ple_gate):
    B, S, _ = x.shape
    split_at = np.cumsum(PROJ_WIDTHS)[:-1].tolist()
    for i in range(DEPTH):
        h = rms_norm(x, g_pre[i]).astype(x.dtype)
        proj = h @ w_in[i]
        (q, k_cmp, v_cmp, k_sel, v_sel, k_win, v_win, nsa_g, z_a,
         u, v, z_b, glu_a, glu_b, z_c, rw, z_d) = jnp.split(proj, split_at, axis=-1)
        kc = nsa_compress(k_cmp, nsa_pos[i, 0], nsa_w1[i, 0], nsa_w2[i, 0])
        vc = nsa_compress(v_cmp, nsa_pos[i, 1], nsa_w1[i, 1], nsa_w2[i, 1])
        y_a = nsa_attention(q.reshape(B, S, N_HEADS, HEAD_DIM), kc, vc, k_sel, v_sel, k_win, v_win,
                            nsa_g.reshape(B, S, N_HEADS, 3), rel_bias)
        y_b = spatial_gating(u, v, sgu_ln_g[i], sgu_ln_b[i], sgu_w[i], sgu_b[i])
        y_c = conv_module(glu_a, glu_b, conv_w[i], conv_b[i], conv_ln_g[i], conv_ln_b[i],
                          conv_pw[i], conv_pw_b[i])
        y_d = rwkv7_time_mix(token_shift(rw, rwkv_mu[i]), rwkv_w0[i], rwkv_w_up[i], rwkv_a0[i],
                             rwkv_a_up[i], rwkv_k_k[i], rwkv_k_a[i], rwkv_r_k[i],
                             rwkv_gn_g[i], rwkv_gn_b[i])
        mix = jnp.concatenate([y_a * jax.nn.silu(z_a), y_b * jax.nn.silu(z_b),
                               y_c * jax.nn.silu(z_c), y_d * jax.nn.silu(z_d)], axis=-1).astype(x.dtype)
        x = x + rms_norm(mix @ w_out[i], g_post[i]).astype(x.dtype)
        x = x + (jax.nn.sigmoid(x @ ple_gate[i]) * (p[i] @ ple_proj[i])).astype(x.dtype)
    return x
```

```python
import math
import numpy as np
import ml_dtypes
import concourse.bass as bass
import concourse.mybir as mybir
from concourse.bass_utils import run_bass_kernel_spmd

F32 = mybir.dt.float32
BF16 = mybir.dt.bfloat16
AF = mybir.ActivationFunctionType
ALU = mybir.AluOpType
AX = mybir.AxisListType

S = 4096
D = 1024
NL = 2
N_IN = 3532
O_Q, O_KC, O_VC, O_KS, O_VS, O_KW, O_VW, O_G, O_ZA = 0, 256, 320, 384, 448, 512, 576, 640, 652
O_U, O_V, O_ZB = 908, 1164, 1420
O_GA, O_GB, O_ZC = 1676, 1932, 2188
O_RW, O_ZD = 2444, 3276
NEGM = -30000.0


class KB:
    COMPUTE = ("pe", "act", "dve", "pool")
    EPOCH = 12000

    def __init__(self, nc, n_dma_sems=24):
        self.nc = nc
        self.E = {"pe": nc.tensor, "act": nc.scalar, "dve": nc.vector, "pool": nc.gpsimd, "sp": nc.sync}
        self.csem = {e: nc.alloc_semaphore(f"c_{e}_0") for e in self.COMPUTE}
        self.cepoch = {e: 0 for e in self.COMPUTE}
        self.ccnt = {e: 0 for e in self.COMPUTE}
        self.waited = {e: {} for e in self.E}
        self.dsems = [nc.alloc_semaphore(f"dq{i}") for i in range(n_dma_sems)]
        self.dcnt = [0] * n_dma_sems
        self.dnext = 0
        self.lastw = {}
        self.readers = {}
        self.semobj = {}
        self.n_ins = 0
        self.all_tokens = {}
        self.relax = False

    def _wait(self, e, tok):
        key, val, src = tok
        if src == "pe" and e == "pe":
            return
        if self.relax and src == e:
            return
        w = self.waited[e]
        if w.get(key, 0) >= val:
            return
        self.E[e].wait_ge(self.semobj[key], val)
        w[key] = val
        self.n_ins += 1

    def _deps(self, e, R, W):
        for r in R:
            t = self.lastw.get(r)
            if t is not None:
                self._wait(e, t)
        for w_ in W:
            t = self.lastw.get(w_)
            if t is not None:
                self._wait(e, t)
            for t in self.readers.get(w_, {}).values():
                self._wait(e, t)

    def _record(self, tok, R, W):
        for r in R:
            self.readers.setdefault(r, {})[tok[0]] = tok
        for w_ in W:
            self.lastw[w_] = tok
            self.readers[w_] = {}
        self.all_tokens[tok[0]] = tok

    def op(self, e, ins_fn, R=(), W=(), relax=False):
        self.relax = relax
        self._deps(e, R, W)
        self.relax = False
        if self.ccnt[e] >= self.EPOCH:
            self.cepoch[e] += 1
            self.csem[e] = self.nc.alloc_semaphore(f"c_{e}_{self.cepoch[e]}")
            self.ccnt[e] = 0
        ins = ins_fn(self.E[e])
        self.ccnt[e] += 1
        key = f"c_{e}_{self.cepoch[e]}"
        self.semobj[key] = self.csem[e]
        ins.then_inc(self.csem[e], 1)
        tok = (key, self.ccnt[e], e)
        self._record(tok, R, W)
        self.n_ins += 1
        return tok

    def dma(self, q, out, in_, R=(), W=(), **kw):
        self._deps(q, R, W)
        if q == "pool":
            self.nsw = getattr(self, "nsw", 0) + 1
            key = f"dsw{self.nsw}"
            sem = self.nc.alloc_semaphore(key)
            self.semobj[key] = sem
            self.E[q].dma_start(out=out, in_=in_, **kw).then_inc(sem, 16)
            tok = (key, 16, "dma")
            self._record(tok, R, W)
            self.n_ins += 1
            return tok
        i = self.dnext
        self.dnext = (self.dnext + 1) % len(self.dsems)
        key = f"dq{i}"
        self.semobj[key] = self.dsems[i]
        if self.dcnt[i] > 0:
            self._wait(q, (key, 16 * self.dcnt[i], "dma"))
        self.E[q].dma_start(out=out, in_=in_, **kw).then_inc(self.dsems[i], 16)
        self.dcnt[i] += 1
        tok = (key, 16 * self.dcnt[i], "dma")
        self._record(tok, R, W)
        self.n_ins += 1
        return tok

    def barrier(self):
        toks = list(self.all_tokens.values())
        for e in self.E:
            for t in toks:
                if t[2] == e:
                    continue
                self._wait(e, t)
        for e in self.COMPUTE:
            if e == "pe":
                continue
            for t in toks:
                if t[2] == e:
                    self._wait(e, t)
        self.lastw = {}
        self.readers = {}

    def finish(self):
        toks = list(self.all_tokens.values())
        for t in toks:
            self._wait("sp", t)


class Arena:
    def __init__(self, nc, base, limit):
        self.nc, self.base, self.limit = nc, base, limit
        self.off = base
        self.uid = 0

    def reset(self, to=None):
        self.off = self.base if to is None else to

    def mark(self):
        return self.off

    def alloc(self, name, shape, dtype):
        nbytes = int(np.prod(shape[1:])) * (4 if dtype == F32 else 2)
        nbytes = (nbytes + 31) // 32 * 32
        assert self.off + nbytes <= self.limit, f"SBUF arena overflow for {name}: {self.off}+{nbytes}>{self.limit}"
        self.uid += 1
        t = self.nc.alloc_sbuf_tensor_at(f"{name}_{self.uid}", list(shape), dtype, offset=self.off)
        self.off += nbytes
        return t


class Pack:
    def __init__(self, rows):
        self.rows = rows
        self.parts = []
        self.off = {}
        self.n = 0

    def add(self, name, arr):
        arr = np.ascontiguousarray(arr, dtype=np.float32)
        assert arr.shape[0] == self.rows, (name, arr.shape)
        arr = arr.reshape(self.rows, -1)
        self.off[name] = (self.n, arr.shape[1])
        self.parts.append(arr)
        self.n += arr.shape[1]

    def build(self):
        return np.concatenate(self.parts, axis=1)


def bc_rows(v, rows=128):
    v = np.asarray(v, dtype=np.float32).reshape(1, -1)
    return np.broadcast_to(v, (rows, v.shape[1]))


def col_chunks(v):
    v = np.asarray(v, dtype=np.float32)
    return v.reshape(-1, 128).T


def rel_bucket_np(dist):
    n = np.maximum(dist, 0)
    nf = np.maximum(n, 16).astype(np.float32)
    large = 16 + (np.log(nf / np.float32(16)) / np.float32(math.log(128 / 16)) * np.float32(16)).astype(np.int32)
    large = np.minimum(large, 31)
    return np.where(n < 16, n, large)


class Ctx:
    pass


def mm(kb, out, lhsT, rhs, start, stop, R, W):
    kb.op("pe", lambda e: e.matmul(out, lhsT, rhs, start=start, stop=stop), R=R, W=W)


def build_program(bc_off, col_off, n_bc, n_col, phases=("A", "N", "B", "C", "R", "E"), debug=False, nlayers=NL, layers=None):
    nc = bass.Bass("TRN2", target_bir_lowering=False)
    kb = KB(nc)
    C = Ctx()
    C.nc, C.kb = nc, kb
    C.bc_off, C.col_off = bc_off, col_off
    C.debug = debug
    din = lambda name, shape, dt=F32: nc.dram_tensor(name, list(shape), dt, kind="ExternalInput").ap()
    C.x_in = din("x", [S, D])
    C.p_in = din("p", [NL, S, 256])
    C.w_in = din("w_in", [NL, D, N_IN])
    C.w_out = din("w_out", [NL, D, D])
    C.ple_gate = din("ple_gate", [NL, D, D])
    C.ple_proj = din("ple_proj", [NL, 256, D])
    C.bc = din("bc", [NL, 128, n_bc])
    C.col = din("col", [NL, 128, n_col])
    C.gpp = din("gpp", [NL, 128, 2 * D])
    C.cst = din("cst", [128, 1024])
    C.sgu_wT = din("sgu_wT", [NL, 128, 4, 128])
    C.conv_pw = din("conv_pw", [NL, 256, 256])
    C.nsa_w1 = din("nsa_w1", [NL, 2, 2048, 128])
    C.nsa_w2 = din("nsa_w2", [NL, 2, 128, 64])
    C.nsa_posT = din("nsa_posT", [NL, 2, 64, 32])
    C.xbig = din("xbig", [64, S])
    C.gs = din("gs", [128, 4, 1024])
    C.gw = din("gw", [128, 4, 1408])
    C.mmat = din("mmat", [256, 64])
    C.tk_mul = din("tk_mul", [32, 128, 64])
    C.tk_add = din("tk_add", [32, 128, 64])
    C.cbias = din("cbias", [4, 256, S])
    C.rwkv_w_up = din("rwkv_w_up", [NL, 32, 256])
    C.rwkv_a_up = din("rwkv_a_up", [NL, 32, 256])
    C.rmasks = din("rmasks", [64, 4, 512])
    if debug:
        C.dbg_yd = nc.dram_tensor("dbg_yd", [S, 256], F32, kind="ExternalOutput").ap()
        C.dbg_sel = nc.dram_tensor("dbg_sel", [S, 64], F32, kind="ExternalOutput").ap()
        C.dbg_ya = nc.dram_tensor("dbg_ya", [S, 256], F32, kind="ExternalOutput").ap()
        C.dbg_y3 = nc.dram_tensor("dbg_y3", [2, S, 256], F32, kind="ExternalOutput").ap()
    C.out = nc.dram_tensor("out", [S, D], F32, kind="ExternalOutput").ap()
    C.xmid = nc.dram_tensor("xmid", [S, D], F32).ap()
    mk = "ExternalOutput" if debug else "Internal"
    C.mixT = nc.dram_tensor("mixT", [8, 128, S], BF16, kind=mk).ap()

    st = Arena(nc, 16640, 229376)
    C.hT = st.alloc("hT", [128, 8, S], BF16)
    C.identb = st.alloc("identb", [128, 128], BF16)
    C.cstf = st.alloc("cstf", [128, 1024], F32)
    C.bcv = st.alloc("bcv", [128, n_bc], F32)
    C.colv = st.alloc("colv", [128, n_col], F32)
    C.arena = Arena(nc, st.off, 229376)
    C.ps = [nc.alloc_psum_tensor(f"ps{i}", [128, 512], F32) for i in range(6)]
    C.psb = [nc.alloc_psum_tensor(f"psb{i}", [128, 1024], BF16) for i in range(2)]

    kb.dma("sp", out=C.cstf[:], in_=C.cst[:, :], W=["cstf"])
    kb.op("dve", lambda e: e.tensor_copy(C.identb[:], C.cstf[:, 0:128]), R=["cstf"], W=["identb"])

    if layers is None:
        layers = tuple(range(nlayers))
    for li, l in enumerate(layers):
        xsrc = C.x_in if li == 0 else C.xmid
        xdst = C.out if li == len(layers) - 1 else C.xmid
        kb.dma("sp", out=C.bcv[:], in_=C.bc[l], W=["bcv"])
        kb.dma("sp", out=C.colv[:], in_=C.col[l], W=["colv"])
        if "A" in phases:
            phase_ABC(C, l, xsrc)
        if "N" in phases:
            kb.barrier(); C.arena.reset()
            phase_N(C, l)
        if "R" in phases:
            kb.barrier(); C.arena.reset()
            phase_R(C, l)
        if "E" in phases:
            kb.barrier(); C.arena.reset()
            phase_E(C, l, xsrc, xdst)
        kb.barrier(); C.arena.reset()
    kb.finish()
    return nc, kb


def rsqrt_op(C, out, in_, scale, eps, R, W):
    kb = C.kb
    kb.op("dve", lambda e: e.tensor_scalar(out=out, in0=in_, scalar1=scale, scalar2=eps, op0=ALU.mult, op1=ALU.add), R=R, W=W)
    kb.op("act", lambda e: e.activation(out=out, in_=out, func=AF.Sqrt), R=W, W=W)
    kb.op("dve", lambda e: e.reciprocal(out=out, in_=out), R=W, W=W)


def bcs(C, name):
    o, n = C.bc_off[name]
    return C.bcv[:, o:o + n]


def cols(C, name, j=0, n=1):
    o, _ = C.col_off[name]
    return C.colv[:, o + j:o + j + n]


def load_w_bf16(C, dst, src_ap, res, q="pool"):
    C.kb.dma(q, out=dst, in_=src_ap.rearrange("(k p) c -> p k c", p=128), W=[res])


def gelu_tanh(C, out, src_ps, t1, sg, R, W, n):
    kb = C.kb
    kb.op("act", lambda e: e.activation(out=t1, in_=src_ps, func=AF.Square), R=R, W=[n + "t1"])
    kb.op("dve", lambda e: e.tensor_scalar(out=t1, in0=t1, scalar1=0.044715, scalar2=1.0, op0=ALU.mult, op1=ALU.add),
          R=[n + "t1"], W=[n + "t1"])
    kb.op("dve", lambda e: e.tensor_tensor(out=t1, in0=src_ps, in1=t1, op=ALU.mult), R=R + [n + "t1"], W=[n + "t1"])
    kb.op("act", lambda e: e.activation(out=sg, in_=t1, func=AF.Sigmoid, scale=1.5957691216057308), R=[n + "t1"], W=[n + "sg"])
    kb.op("dve", lambda e: e.tensor_tensor(out=out, in0=src_ps, in1=sg, op=ALU.mult), R=R + [n + "sg"], W=W)


def gen_A(C, l, xsrc, prog):
    kb = C.kb
    ar = C.arena
    sq = ar.alloc("sq", [128, D], F32)
    hb = [ar.alloc(f"hb{i}", [128, D], BF16) for i in range(2)]
    ss = ar.alloc("ss", [128, 4], F32)
    gpre_t = ar.alloc("gpre", [128, D], F32)
    kb.dma("sp", out=gpre_t[:], in_=C.gpp[l][:, 0:D], W=["bcv"])
    gpre = gpre_t[:]
    xins = [ar.alloc(f"xin{i}", [128, D], F32) for i in range(2)]
    for tt in range(32):
        s_ = tt % 2
        xin = xins[s_]
        tok = slice(tt * 128, (tt + 1) * 128)
        kb.dma("sp", out=xin[:], in_=xsrc[tok, :], W=[f"xin{s_}"])
        kb.op("act", lambda e: e.activation(out=sq[:], in_=xin[:], func=AF.Square), R=[f"xin{s_}"], W=["sq"])
        kb.op("dve", lambda e: e.reduce_sum(out=ss[:, s_:s_ + 1], in_=sq[:], axis=AX.X), R=["sq"], W=[f"ss{s_}"])
        rsqrt_op(C, ss[:, 2 + s_:3 + s_], ss[:, s_:s_ + 1], 1.0 / D, 1e-6, [f"ss{s_}"], [f"rs{s_}"])
        kb.op("dve", lambda e: e.scalar_tensor_tensor(out=hb[s_][:], in0=xin[:], scalar=ss[:, 2 + s_:3 + s_], in1=gpre,
                                                      op0=ALU.mult, op1=ALU.mult),
              R=[f"xin{s_}", f"rs{s_}", "bcv"], W=[f"hb{s_}"])
        pb = C.psb[0]
        for k in range(8):
            kb.op("pe", lambda e: e.transpose(pb[:, k * 128:(k + 1) * 128], hb[s_][:, k * 128:(k + 1) * 128], C.identb[:]),
                  R=[f"hb{s_}", "identb"], W=["psb0"])
        kb.op("act", lambda e: e.copy(C.hT[:, :, tok], pb[:, :].rearrange("p (k t) -> p k t", k=8)),
              R=["psb0"], W=[f"hT{tt}"])
        prog[0] = tt + 1
        yield


def gen_B(C, l, prog):
    kb, ar = C.kb, C.arena
    wB = ar.alloc("wB", [128, 8, 768], BF16)
    load_w_bf16(C, wB[:], C.w_in[l][:, O_U:O_U + 768], "wB")
    wsf = ar.alloc("wsf", [128, 4, 128], F32)
    wsb = ar.alloc("wsb", [128, 4, 128], BF16)
    kb.dma("sp", out=wsf[:], in_=C.sgu_wT[l], W=["wsf"])
    trilT = C.cstf[:, 128:256]
    for h in range(4):
        kb.op("dve", lambda e: e.tensor_tensor(out=wsb[:, h, :], in0=wsf[:, h, :], in1=trilT, op=ALU.mult),
              R=["wsf", "cstf"], W=["wsb"])
    t1 = ar.alloc("t1", [128, 512], F32)
    sg = ar.alloc("sg", [128, 512], F32)
    guv = ar.alloc("guv", [128, 512], F32)
    st6 = ar.alloc("st6", [128, 6], F32)
    mv = ar.alloc("mv", [128, 4], F32)
    vn = ar.alloc("vn", [128, 256], F32)
    vnb = ar.alloc("vnb", [128, 256], BF16)
    yb = ar.alloc("yb", [128, 256], F32)
    sz = ar.alloc("sz", [128, 256], F32)
    mb = ar.alloc("mb", [128, 256], BF16)
    mt = [ar.alloc(f"mt{i}", [128, 2, 128], BF16) for i in range(2)]
    lng, lnb = bcs(C, "sgu_ln_g"), bcs(C, "sgu_ln_b")
    mixv = C.mixT.rearrange("k p t -> p k t")
    for tt in range(32):
        while prog[0] < tt + 1:
            yield
        tok = slice(tt * 128, (tt + 1) * 128)
        pu, pz, pv = C.ps[0], C.ps[1], C.ps[2]
        for k in range(8):
            mm(kb, pu[:, :], C.hT[:, k, tok], wB[:, k, 0:512], k == 0, k == 7, [f"hT{tt}", "wB"], ["ps0"])
        for k in range(8):
            mm(kb, pz[:, 0:256], C.hT[:, k, tok], wB[:, k, 512:768], k == 0, k == 7, [f"hT{tt}", "wB"], ["ps1"])
        yield
        gelu_tanh(C, guv[:], pu[:, :], t1[:], sg[:], ["ps0"], ["guv"], "B")
        yield
        kb.op("dve", lambda e: e.bn_stats(out=st6[:], in_=guv[:, 256:512]), R=["guv"], W=["st6"])
        kb.op("dve", lambda e: e.bn_aggr(out=mv[:, 0:2], in_=st6[:]), R=["st6"], W=["mv"])
        rsqrt_op(C, mv[:, 2:3], mv[:, 1:2], 1.0, 1e-5, ["mv"], ["mvr"])
        kb.op("dve", lambda e: e.tensor_scalar(out=vn[:], in0=guv[:, 256:512], scalar1=mv[:, 0:1], scalar2=mv[:, 2:3],
                                               op0=ALU.subtract, op1=ALU.mult), R=["guv", "mv", "mvr"], W=["vn"])
        kb.op("dve", lambda e: e.tensor_tensor(out=vn[:], in0=vn[:], in1=lng, op=ALU.mult), R=["vn", "bcv"], W=["vn"])
        kb.op("dve", lambda e: e.tensor_tensor(out=vnb[:], in0=vn[:], in1=lnb, op=ALU.add), R=["vn", "bcv"], W=["vnb"])
        yield
        for h in range(4):
            hs = slice(h * 64, (h + 1) * 64)
            mm(kb, pv[:, hs], wsb[:, h, :], vnb[:, hs], True, True, ["wsb", "vnb"], ["ps2"])
        for h in range(4):
            hs = slice(h * 64, (h + 1) * 64)
            kb.op("dve", lambda e: e.scalar_tensor_tensor(out=yb[:, hs], in0=pv[:, hs], scalar=cols(C, "sgu_bT", h),
                                                          in1=guv[:, hs], op0=ALU.add, op1=ALU.mult),
                  R=["ps2", "colv", "guv"], W=["yb"])
        kb.op("act", lambda e: e.activation(out=sz[:], in_=pz[:, 0:256], func=AF.Silu), R=["ps1"], W=["sz"])
        kb.op("dve", lambda e: e.tensor_tensor(out=mb[:], in0=yb[:], in1=sz[:], op=ALU.mult), R=["yb", "sz"], W=["mb"])
        yield
        s_ = tt % 2
        pb = C.psb[1]
        for j in range(2):
            kb.op("pe", lambda e: e.transpose(pb[:, j * 128:(j + 1) * 128], mb[:, j * 128:(j + 1) * 128], C.identb[:]),
                  R=["mb", "identb"], W=["psb1"])
        kb.op("act", lambda e: e.copy(mt[s_][:], pb[:, 0:256].rearrange("p (k t) -> p k t", k=2)),
              R=["psb1"], W=[f"mt{s_}"])
        kb.dma("sp", out=mixv[:, 2:4, tok], in_=mt[s_][:], R=[f"mt{s_}"], W=["mixT_B"])
        yield


def gen_C(C, l, prog):
    kb, ar = C.kb, C.arena
    wC = ar.alloc("wC", [128, 8, 768], BF16)
    load_w_bf16(C, wC[:], C.w_in[l][:, O_GA:O_GA + 768], "wC")
    wpw = ar.alloc("wpw", [128, 2, 256], BF16)
    load_w_bf16(C, wpw[:], C.conv_pw[l], "wpw")
    xg = [ar.alloc(f"xg{j}", [128, 544], F32) for j in range(2)]
    acc = [ar.alloc(f"acc{j}", [128, 512], F32) for j in range(2)]
    acc2 = [ar.alloc(f"accp{j}", [128, 512], F32) for j in range(2)]
    tmpA = [ar.alloc(f"tmpA{j}", [128, 512], F32) for j in range(2)]
    sqc = [ar.alloc(f"sqc{j}", [128, 512], F32) for j in range(2)]
    szc = [ar.alloc(f"szc{j}", [128, 512], F32) for j in range(2)]
    yn = [ar.alloc(f"yn{j}", [128, 512], BF16) for j in range(2)]
    sig = ar.alloc("sig", [128, 512], F32)
    m2 = ar.alloc("m2", [128, 512], F32)
    rstd = ar.alloc("rstd", [128, 512], F32)
    tmp = ar.alloc("tmpc", [128, 512], F32)
    mc = [ar.alloc(f"mc{j}", [128, 512], BF16) for j in range(2)]
    onesm = C.cstf[:, 256:384]
    for j in range(2):
        kb.op("dve", lambda e: e.memset(xg[j][:, 0:30], 0.0), W=[f"xg{j}"])
    for T in range(8):
        while prog[0] < 4 * T + 4:
            yield
        tok = slice(T * 512, (T + 1) * 512)
        hres = [f"hT{4 * T + i_}" for i_ in range(4)]
        for j in range(2):
            pa, pb_, pz = C.ps[3], C.ps[4], C.ps[5]
            for (pp, c0, rn) in ((pa, j * 128, "ps3"), (pb_, 256 + j * 128, "ps4"), (pz, 512 + j * 128, "ps5")):
                for k in range(8):
                    mm(kb, pp[:, :], wC[:, k, c0:c0 + 128], C.hT[:, k, tok], k == 0, k == 7, hres + ["wC"], [rn])
            yield
            kb.op("act", lambda e: e.activation(out=sig[:], in_=pb_[:, :], func=AF.Sigmoid), R=["ps4"], W=["sig"])
            kb.op("dve", lambda e: e.tensor_tensor(out=xg[j][:, 30:542], in0=pa[:, :], in1=sig[:], op=ALU.mult),
                  R=["ps3", "sig"], W=[f"xg{j}"])
            kb.op("act", lambda e: e.activation(out=szc[j][:], in_=pz[:, :], func=AF.Silu), R=["ps5"], W=[f"szc{j}"])
            kb.op("dve", lambda e: e.tensor_scalar(out=acc[j][:], in0=xg[j][:, 0:512], scalar1=cols(C, "conv_wT", j * 31),
                                                   scalar2=cols(C, "conv_b", j), op0=ALU.mult, op1=ALU.add),
                  R=[f"xg{j}", "colv"], W=[f"acc{j}"])
            for jj in range(1, 31):
                kb.op("dve", lambda e: e.scalar_tensor_tensor(out=acc[j][:], in0=xg[j][:, jj:jj + 512],
                                                              scalar=cols(C, "conv_wT", j * 31 + jj), in1=acc[j][:],
                                                              op0=ALU.mult, op1=ALU.add),
                      R=[f"xg{j}", "colv", f"acc{j}"], W=[f"acc{j}"])
                if jj % 4 == 0:
                    yield
            kb.op("dve", lambda e: e.tensor_copy(xg[j][:, 0:30], xg[j][:, 512:542]), R=[f"xg{j}"], W=[f"xg{j}"])
            kb.op("act", lambda e: e.activation(out=sqc[j][:], in_=acc[j][:], func=AF.Square), R=[f"acc{j}"], W=[f"sqc{j}"])
        yield
        pm, pq = C.ps[3], C.ps[4]
        for j in range(2):
            mm(kb, pm[:, :], onesm, acc[j][:], j == 0, j == 1, ["cstf", f"acc{j}"], ["ps3"])
        for j in range(2):
            mm(kb, pq[:, :], onesm, sqc[j][:], j == 0, j == 1, ["cstf", f"sqc{j}"], ["ps4"])
        kb.op("act", lambda e: e.activation(out=m2[:], in_=pm[:, :], func=AF.Square), R=["ps3"], W=["m2"])
        kb.op("dve", lambda e: e.tensor_tensor(out=rstd[:], in0=pq[:, :], in1=m2[:], op=ALU.subtract), R=["ps4", "m2"], W=["rstd"])
        rsqrt_op(C, rstd[:], rstd[:], 1.0, 1e-5, ["rstd"], ["rstd"])
        for j in range(2):
            kb.op("dve", lambda e: e.tensor_tensor(out=tmp[:], in0=acc[j][:], in1=pm[:, :], op=ALU.subtract),
                  R=[f"acc{j}", "ps3"], W=["tmpc"])
            kb.op("dve", lambda e: e.tensor_tensor(out=tmp[:], in0=tmp[:], in1=rstd[:], op=ALU.mult), R=["tmpc", "rstd"], W=["tmpc"])
            kb.op("act", lambda e: e.activation(out=yn[j][:], in_=tmp[:], func=AF.Silu, bias=cols(C, "conv_ln_b", j),
                                                scale=cols(C, "conv_ln_g", j)), R=["tmpc", "colv"], W=[f"yn{j}"])
        yield
        mixv = C.mixT
        for jo in range(2):
            po = C.ps[5]
            for j in range(2):
                mm(kb, po[:, :], wpw[:, j, jo * 128:(jo + 1) * 128], yn[j][:], j == 0, j == 1, ["wpw", f"yn{j}"], ["ps5"])
            kb.op("dve", lambda e: e.scalar_tensor_tensor(out=mc[jo][:], in0=po[:, :], scalar=cols(C, "conv_pw_b", jo),
                                                          in1=szc[jo][:], op0=ALU.add, op1=ALU.mult),
                  R=["ps5", "colv", f"szc{jo}"], W=[f"mc{jo}"])
            kb.dma("sp", out=mixv[4 + jo, :, tok], in_=mc[jo][:], R=[f"mc{jo}"], W=["mixT_C"])
        yield


def run_interleaved(*gens):
    live = [g_ for g_ in gens if g_ is not None]
    while live:
        for g_ in list(live):
            try:
                next(g_)
            except StopIteration:
                live.remove(g_)


def phase_ABC(C, l, xsrc):
    prog = [0]
    run_interleaved(gen_A(C, l, xsrc, prog), gen_B(C, l, prog), gen_C(C, l, prog))


def phase_E(C, l, xsrc, xdst):
    kb, ar = C.kb, C.arena
    wo = ar.alloc("wo", [128, 8, D], BF16)
    wg = ar.alloc("wg", [128, 8, D], BF16)
    wp = ar.alloc("wp", [128, 2, D], BF16)
    load_w_bf16(C, wo[:], C.w_out[l], "wo")
    load_w_bf16(C, wg[:], C.ple_gate[l], "wg")
    load_w_bf16(C, wp[:], C.ple_proj[l], "wp")
    mt = [ar.alloc(f"mtE{i}", [128, 8, 512], BF16) for i in range(2)]
    osbs = [ar.alloc(f"osb{i}", [128, D], F32) for i in range(2)]
    sq = ar.alloc("sqE", [128, D], F32)
    x1 = ar.alloc("x1", [128, D], F32)
    x1b = ar.alloc("x1b", [128, D], BF16)
    x1T = ar.alloc("x1T", [128, 8, 128], BF16)
    pins = [ar.alloc(f"pin{i}", [128, 256], F32) for i in range(2)]
    pbf = ar.alloc("pbf", [128, 256], BF16)
    pT = ar.alloc("pT", [128, 2, 128], BF16)
    sgs = ar.alloc("sgs", [128, D], F32)
    x2 = [ar.alloc(f"x2{i}", [128, D], F32) for i in range(2)]
    ss = ar.alloc("ssE", [128, 4], F32)
    gpost_t = ar.alloc("gpost", [128, D], F32)
    kb.dma("sp", out=gpost_t[:], in_=C.gpp[l][:, D:2 * D], W=["bcv"])
    gpost = gpost_t[:]
    xins = [ar.alloc(f"xin{i}", [128, D], F32) for i in range(2)]
    mixv = C.mixT.rearrange("k p t -> p k t")

    def load_mt(T):
        ms = T % 2
        kb.dma("sp", out=mt[ms][:], in_=mixv[:, :, T * 512:(T + 1) * 512], R=["mixT_A", "mixT_B", "mixT_C", "mixT_D"], W=[f"mtE{ms}"])

    def part_a(tt):
        T, sub = tt // 4, tt % 4
        ms, s_ = T % 2, tt % 2
        tok = slice(tt * 128, (tt + 1) * 128)
        xin, pin, osb = xins[s_], pins[s_], osbs[s_]
        kb.dma("sp", out=xin[:], in_=xsrc[tok, :], W=[f"xin{s_}"])
        kb.dma("sp", out=pin[:], in_=C.p_in[l, tok, :], W=[f"pin{s_}"])
        for half in range(2):
            po = C.ps[half]
            for k in range(8):
                mm(kb, po[:, :], mt[ms][:, k, sub * 128:(sub + 1) * 128], wo[:, k, half * 512:(half + 1) * 512],
                   k == 0, k == 7, [f"mtE{ms}", "wo"], [f"ps{half}"])
            kb.op("act", lambda e: e.copy(osb[:, half * 512:(half + 1) * 512], po[:, :]), R=[f"ps{half}"], W=[f"osb{s_}"])

    def part_b(tt):
        s_ = tt % 2
        tok = slice(tt * 128, (tt + 1) * 128)
        xin, pin, osb = xins[s_], pins[s_], osbs[s_]
        osn, pinn = f"osb{s_}", f"pin{s_}"
        kb.op("act", lambda e: e.activation(out=sq[:], in_=osb[:], func=AF.Square), R=[osn], W=["sqE"])
        kb.op("dve", lambda e: e.reduce_sum(out=ss[:, 0:1], in_=sq[:], axis=AX.X), R=["sqE"], W=["ssE"])
        rsqrt_op(C, ss[:, 1:2], ss[:, 0:1], 1.0 / D, 1e-6, ["ssE"], ["ssE"])
        kb.op("dve", lambda e: e.scalar_tensor_tensor(out=x1[:], in0=osb[:], scalar=ss[:, 1:2], in1=gpost,
                                                      op0=ALU.mult, op1=ALU.mult), R=[osn, "ssE", "bcv"], W=["x1"])
        kb.op("dve", lambda e: e.tensor_tensor(out=x1[:], in0=x1[:], in1=xin[:], op=ALU.add), R=["x1", f"xin{s_}"], W=["x1"])
        kb.op("act", lambda e: e.copy(x1b[:], x1[:]), R=["x1"], W=["x1b"])
        kb.op("act", lambda e: e.copy(pbf[:], pin[:]), R=[pinn], W=["pbf"])
        pb = C.psb[0]
        for k in range(8):
            kb.op("pe", lambda e: e.transpose(pb[:, k * 128:(k + 1) * 128], x1b[:, k * 128:(k + 1) * 128], C.identb[:]),
                  R=["x1b", "identb"], W=["psb0"])
        kb.op("dve", lambda e: e.tensor_copy(x1T[:], pb[:, :].rearrange("p (k t) -> p k t", k=8)), R=["psb0"], W=["x1T"])
        pb2 = C.psb[1]
        for k in range(2):
            kb.op("pe", lambda e: e.transpose(pb2[:, k * 128:(k + 1) * 128], pbf[:, k * 128:(k + 1) * 128], C.identb[:]),
                  R=["pbf", "identb"], W=["psb1"])
        kb.op("dve", lambda e: e.tensor_copy(pT[:], pb2[:, 0:256].rearrange("p (k t) -> p k t", k=2)), R=["psb1"], W=["pT"])
        for half in range(2):
            pg, pe_ = C.ps[2 + half], C.ps[4 + half]
            hs = slice(half * 512, (half + 1) * 512)
            for k in range(8):
                mm(kb, pg[:, :], x1T[:, k, :], wg[:, k, hs], k == 0, k == 7, ["x1T", "wg"], [f"ps{2 + half}"])
            for k in range(2):
                mm(kb, pe_[:, :], pT[:, k, :], wp[:, k, hs], k == 0, k == 1, ["pT", "wp"], [f"ps{4 + half}"])
            kb.op("act", lambda e: e.activation(out=sgs[:, hs], in_=pg[:, :], func=AF.Sigmoid), R=[f"ps{2 + half}"], W=["sgs"])
            kb.op("dve", lambda e: e.tensor_tensor(out=x2[s_][:, hs], in0=pe_[:, :], in1=sgs[:, hs], op=ALU.mult),
                  R=[f"ps{4 + half}", "sgs"], W=[f"x2{s_}"])
        kb.op("dve", lambda e: e.tensor_tensor(out=x2[s_][:], in0=x2[s_][:], in1=x1[:], op=ALU.add), R=[f"x2{s_}", "x1"], W=[f"x2{s_}"])
        kb.dma("sp", out=xdst[tok, :], in_=x2[s_][:], R=[f"x2{s_}"], W=["xdst"])

    load_mt(0)
    part_a(0)
    for tt in range(32):
        if tt + 1 < 32:
            if (tt + 1) % 4 == 0:
                load_mt((tt + 1) // 4)
            part_a(tt + 1)
        part_b(tt)


def phase_N(C, l):
    kb, ar = C.kb, C.arena
    EXP = AF.Exp
    wq = ar.alloc("wq", [128, 8, 256], BF16)
    wkv = ar.alloc("wkv", [128, 8, 384], BF16)
    wgz = ar.alloc("wgz", [128, 8, 268], BF16)
    load_w_bf16(C, wq[:], C.w_in[l][:, O_Q:O_Q + 256], "wq")
    load_w_bf16(C, wkv[:], C.w_in[l][:, O_KC:O_KC + 384], "wkv")
    load_w_bf16(C, wgz[:], C.w_in[l][:, O_G:O_G + 268], "wgz")
    kselT = ar.alloc("kselT", [64, S], BF16)
    kwinT = ar.alloc("kwinT", [64, S], BF16)
    vaugS = ar.alloc("vaugS", [128, 32, 65], BF16)
    vaugW = ar.alloc("vaugW", [128, 32, 65], BF16)
    kcT = ar.alloc("kcT", [64, 256], BF16)
    vcaug = ar.alloc("vcaug", [128, 2, 129], BF16)
    Xbig = ar.alloc("Xbig", [64, S], BF16)
    Gs = ar.alloc("Gs", [128, 4, 1024], F32)
    Gw = ar.alloc("Gw", [128, 4, 1408], F32)
    kb.dma("pool", out=Xbig[:], in_=C.xbig[:, :], W=["Xbig"])
    kb.dma("sp", out=Gs[:], in_=C.gs[:, :, :], W=["Gs"])
    kb.dma("sp", out=Gw[:], in_=C.gw[:, :, :], W=["Gw"])
    kb.op("dve", lambda e: e.memset(vcaug[:], 0.0), W=["vcaug"])
    kb.op("dve", lambda e: e.memset(vcaug[:, :, 64:65], 1.0), W=["vcaug"])
    kb.dma("pool", out=vcaug[:, :, 65:129], in_=C.mmat.rearrange("(t p) j -> p t j", p=128), R=["vcaug"], W=["vcaug"])
    kb.op("dve", lambda e: e.memset(vaugS[:, :, 64:65], 1.0), W=["vaugS"])
    kb.op("dve", lambda e: e.memset(vaugW[:, :, 64:65], 1.0), W=["vaugW"])
    mark = ar.mark()
    kcmpT = ar.alloc("kcmpT", [64, S], BF16)
    vcmpT = ar.alloc("vcmpT", [64, S], BF16)
    cnt = 0
    for (dst, c0, rn) in ((kcmpT, 0, "kcmpT"), (vcmpT, 64, "vcmpT"), (kselT, 128, "kselT"), (kwinT, 256, "kwinT")):
        for T in range(8):
            b = cnt % 2; cnt += 1
            pp = C.ps[b]
            for k in range(8):
                mm(kb, pp[0:64, :], wkv[:, k, c0:c0 + 64], C.hT[:, k, T * 512:(T + 1) * 512], k == 0, k == 7, ["hT", "wkv"], [f"ps{b}"])
            eng = "act" if b == 0 else "dve"
            if eng == "act":
                kb.op("act", lambda e: e.copy(dst[:, T * 512:(T + 1) * 512], pp[0:64, :]), R=[f"ps{b}"], W=[rn])
            else:
                kb.op("dve", lambda e: e.tensor_copy(dst[:, T * 512:(T + 1) * 512], pp[0:64, :]), R=[f"ps{b}"], W=[rn])
    for tt in range(32):
        b = tt % 2
        pp = C.ps[2 + b]
        tok = slice(tt * 128, (tt + 1) * 128)
        for (c0, o0) in ((192, 0), (320, 64)):
            for k in range(8):
                mm(kb, pp[:, o0:o0 + 64], C.hT[:, k, tok], wkv[:, k, c0:c0 + 64], k == 0, k == 7, ["hT", "wkv"], [f"ps{2 + b}"])
        kb.op("act", lambda e: e.copy(vaugS[:, tt, 0:64], pp[:, 0:64]), R=[f"ps{2 + b}"], W=["vaugS"])
        kb.op("dve", lambda e: e.tensor_copy(vaugW[:, tt, 0:64], pp[:, 64:128]), R=[f"ps{2 + b}"], W=["vaugW"])
    w1b = ar.alloc("w1b", [64, 2, 32, 128], BF16)
    w2b = ar.alloc("w2b", [128, 2, 64], BF16)
    posf = ar.alloc("posf", [64, 2, 32], F32)
    posb = ar.alloc("posb", [64, 2, 32, 2], BF16)
    hid = ar.alloc("hid", [128, 256], BF16)
    bsb = ar.alloc("bsb", [128, 2], F32)
    for kv in range(2):
        kb.dma("pool", out=w1b[:, kv], in_=C.nsa_w1[l, kv].rearrange("(j d) h -> d j h", d=64), W=["w1b"])
        kb.dma("pool", out=w2b[:, kv, :], in_=C.nsa_w2[l, kv], W=["w2b"])
    kb.dma("sp", out=posf[:], in_=C.nsa_posT[l].rearrange("v d j -> d v j"), W=["posf"])
    for dup in range(2):
        kb.op("dve", lambda e: e.tensor_copy(posb[:, :, :, dup], posf[:]), R=["posf"], W=["posb"])
    for kv in range(2):
        src = kcmpT if kv == 0 else vcmpT
        srn = "kcmpT" if kv == 0 else "vcmpT"
        sv = src[:, :].rearrange("p (b s) -> p b s", s=16)
        ph, pbias, po = C.ps[0], C.ps[1], C.ps[4]
        for j in range(32):
            mm(kb, ph[:, 0:255], w1b[:, kv, j, :], sv[:, (j // 16):(j // 16) + 255, j % 16], j == 0, j == 31, ["w1b", srn], ["ps0"])
        for j in range(32):
            mm(kb, pbias[:, 0:2], w1b[:, kv, j, :], posb[:, kv, j, :], j == 0, j == 31, ["w1b", "posb"], ["ps1"])
        kb.op("dve", lambda e: e.tensor_copy(bsb[:, 0:2], pbias[:, 0:2]), R=["ps1"], W=["bsb"])
        kb.op("act", lambda e: e.activation(out=hid[:, 0:255], in_=ph[:, 0:255], func=AF.Silu, bias=bsb[:, 0:1]), R=["ps0", "bsb"], W=["hid"])
        if kv == 0:
            mm(kb, po[0:64, 0:255], w2b[:, 0, :], hid[:, 0:255], True, True, ["w2b", "hid"], ["ps4"])
            kb.op("dve", lambda e: e.tensor_copy(kcT[:, 0:255], po[0:64, 0:255]), R=["ps4"], W=["kcT"])
        else:
            for nt in range(2):
                nn = 128 if nt == 0 else 127
                mm(kb, po[0:nn, nt * 64:(nt + 1) * 64], hid[:, nt * 128:nt * 128 + nn], w2b[:, 1, :], True, True, ["w2b", "hid"], ["ps4"])
                kb.op("dve", lambda e: e.tensor_copy(vcaug[0:nn, nt, 0:64], po[0:nn, nt * 64:(nt + 1) * 64]), R=["ps4"], W=["vcaug"])
    kb.barrier()
    ar.reset(mark)
    qT = ar.alloc("qT", [64, 4, 512], BF16)
    gsb = ar.alloc("gsb", [128, 4, 12], F32)
    sza = ar.alloc("sza", [128, 4, 256], F32)
    acc = ar.alloc("acco", [128, 4, 256], F32)
    imp = ar.alloc("imp", [128, 4, 64], F32)
    imp2 = ar.alloc("imp2", [128, 64], F32)
    tmpk = ar.alloc("tmpk", [128, 64], F32)
    selm = ar.alloc("selm", [128, 64], F32)
    selb = ar.alloc("selb", [128, 64], BF16)
    m8 = ar.alloc("m8", [128, 16], F32)
    tkm = ar.alloc("tkm", [128, 2, 4, 64], F32)
    negT = ar.alloc("negT", [64, 512], BF16)
    cb = [ar.alloc(f"cb{i}", [128, 512], F32) for i in range(2)]
    s2 = [ar.alloc(f"s2{i}", [128, 512], F32) for i in range(2)]
    eT = [ar.alloc(f"eT{i}", [128, 512], BF16) for i in range(4)]
    eC = [ar.alloc(f"eC{i}", [128, 512], BF16) for i in range(2)]
    rd = ar.alloc("rd", [128, 8], F32)
    rd4 = ar.alloc("rd4", [128, 4, 2], F32)
    osb4 = [ar.alloc(f"osb4{i}", [128, 4, 65], F32) for i in range(2)]
    fcnt = [0]
    ma = ar.alloc("ma", [128, 256], BF16)
    mt = [ar.alloc(f"mtN{i}", [128, 2, 128], BF16) for i in range(2)]
    identf = C.cstf[:, 0:128]
    mixv = C.mixT.rearrange("k p t -> p k t")
    sbanks = [(C.ps[0], "ps0"), (C.ps[1], "ps1"), (C.psb[1][:, :].bitcast(F32), "psb1")]
    sc = [0]
    ec = [0]
    oc = [0]
    cbc = [0]
    for Q in range(8):
        qtok = slice(Q * 512, (Q + 1) * 512)
        for h in range(4):
            b = sc[0] % 3; sc[0] += 1
            pp, ppn = sbanks[b]
            for k in range(8):
                mm(kb, pp[0:64, :], wq[:, k, h * 64:(h + 1) * 64], C.hT[:, k, qtok], k == 0, k == 7, ["hT", "wq"], [ppn])
            kb.op("act", lambda e: e.copy(qT[:, h, :], pp[0:64, :]), R=[ppn], W=["qT"])
        for sub in range(4):
            b = sc[0] % 3; sc[0] += 1
            pp, ppn = sbanks[b]
            tok = slice(Q * 512 + sub * 128, Q * 512 + (sub + 1) * 128)
            for k in range(8):
                mm(kb, pp[:, 0:268], C.hT[:, k, tok], wgz[:, k, :], k == 0, k == 7, ["hT", "wgz"], [ppn])
            kb.op("act", lambda e: e.activation(out=gsb[:, sub, :], in_=pp[:, 0:12], func=AF.Sigmoid), R=[ppn], W=["gsb"])
            kb.op("act", lambda e: e.activation(out=sza[:, sub, :], in_=pp[:, 12:268], func=AF.Silu), R=[ppn], W=["sza"])
        kb.dma("sp", out=tkm[:, 0], in_=C.tk_mul[4 * Q:4 * Q + 4].rearrange("c p j -> p c j"), W=["tkm"])
        kb.dma("sp", out=tkm[:, 1], in_=C.tk_add[4 * Q:4 * Q + 4].rearrange("c p j -> p c j"), W=["tkm"])

        def finish_branch(h, gi):
            fs = fcnt[0] % 2; fcnt[0] += 1
            ob4 = osb4[fs]
            o0 = (h % 2) * 128
            for sub in range(4):
                po = C.ps[2 + sub]
                if sub % 2 == 0:
                    kb.op("act", lambda e: e.copy(ob4[:, sub, :], po[:, o0:o0 + 65]), R=[f"ps{2 + sub}"], W=[f"osb4{fs}"])
                else:
                    kb.op("dve", lambda e: e.tensor_copy(ob4[:, sub, :], po[:, o0:o0 + 65]), R=[f"ps{2 + sub}"], W=[f"osb4{fs}"])
            kb.op("dve", lambda e: e.tensor_scalar(out=rd4[:, :, 0:1], in0=ob4[:, :, 64:65], scalar1=1e-30, scalar2=None, op0=ALU.add),
                  R=[f"osb4{fs}"], W=["rd4"])
            kb.op("dve", lambda e: e.reciprocal(out=rd4[:, :, 0:1], in_=rd4[:, :, 0:1]), R=["rd4"], W=["rd4"])
            kb.op("dve", lambda e: e.tensor_tensor(out=rd4[:, :, 1:2], in0=rd4[:, :, 0:1], in1=gsb[:, :, h * 3 + gi:h * 3 + gi + 1], op=ALU.mult),
                  R=["rd4", "gsb"], W=["rd4"])
            hs = slice(h * 64, (h + 1) * 64)
            kb.op("dve", lambda e: e.tensor_tensor(out=ob4[:, :, 0:64], in0=ob4[:, :, 0:64], in1=rd4[:, :, 1:2].to_broadcast([128, 4, 64]), op=ALU.mult),
                  R=[f"osb4{fs}", "rd4"], W=[f"osb4{fs}"])
            kb.op("dve", lambda e: e.tensor_tensor(out=acc[:, :, hs], in0=acc[:, :, hs], in1=ob4[:, :, 0:64], op=ALU.add),
                  R=[f"osb4{fs}", "acco"], W=["acco"])

        nvis = min(255, 32 * Q + 31)
        tiles = [(0, min(128, nvis))] + ([(1, nvis - 128)] if nvis > 128 else [])
        for h in range(4):
            for (nt, nn) in tiles:
                cs = cbc[0] % 2; cbc[0] += 1
                kb.dma("sp", out=cb[cs][0:nn, :], in_=C.cbias[h, nt * 128:nt * 128 + nn, qtok], W=[f"cb{cs}"])
                b = sc[0] % 3; sc[0] += 1
                pp, ppn = sbanks[b]
                mm(kb, pp[0:nn, :], kcT[:, nt * 128:nt * 128 + nn], qT[:, h, :], True, True, ["kcT", "qT"], [ppn])
                kb.op("dve", lambda e: e.scalar_tensor_tensor(out=s2[cs][0:nn, :], in0=pp[0:nn, :], scalar=0.125, in1=cb[cs][0:nn, :],
                                                              op0=ALU.mult, op1=ALU.add), R=[ppn, f"cb{cs}"], W=[f"s2{cs}"])
                kb.op("act", lambda e: e.activation(out=eC[nt][0:nn, :], in_=s2[cs][0:nn, :], func=EXP), R=[f"s2{cs}"], W=[f"eC{nt}"])
            for half in range(2):
                ob = 2 + oc[0] % 2; oc[0] += 1
                po = C.ps[ob]
                for s_i in range(2):
                    sub = half * 2 + s_i
                    for ti, (nt, nn) in enumerate(tiles):
                        mm(kb, po[:, s_i * 129:(s_i + 1) * 129], eC[nt][0:nn, sub * 128:(sub + 1) * 128], vcaug[0:nn, nt, :],
                           ti == 0, ti == len(tiles) - 1, [f"eC{nt}", "vcaug"], [f"ps{ob}"])
                for s_i in range(2):
                    sub = half * 2 + s_i
                    o0 = s_i * 129
                    kb.op("dve", lambda e: e.tensor_scalar(out=rd[:, 0:1], in0=po[:, o0 + 64:o0 + 65], scalar1=1e-30, scalar2=None, op0=ALU.add),
                          R=[f"ps{ob}"], W=["rd"])
                    kb.op("dve", lambda e: e.reciprocal(out=rd[:, 0:1], in_=rd[:, 0:1]), R=["rd"], W=["rd"])
                    kb.op("dve", lambda e: e.tensor_tensor(out=rd[:, 1:2], in0=rd[:, 0:1], in1=gsb[:, sub, h * 3:h * 3 + 1], op=ALU.mult),
                          R=["rd", "gsb"], W=["rd"])
                    hs = slice(h * 64, (h + 1) * 64)
                    kb.op("dve", lambda e: e.tensor_scalar(out=acc[:, sub, hs], in0=po[:, o0:o0 + 64], scalar1=rd[:, 1:2], scalar2=None, op0=ALU.mult),
                          R=[f"ps{ob}", "rd"], W=["acco"])
                    if h == 0:
                        kb.op("dve", lambda e: e.tensor_scalar(out=imp[:, sub, :], in0=po[:, o0 + 65:o0 + 129], scalar1=rd[:, 0:1], scalar2=None, op0=ALU.mult),
                              R=[f"ps{ob}", "rd"], W=["imp"])
                    else:
                        kb.op("dve", lambda e: e.scalar_tensor_tensor(out=imp[:, sub, :], in0=po[:, o0 + 65:o0 + 129], scalar=rd[:, 0:1], in1=imp[:, sub, :],
                                                                      op0=ALU.mult, op1=ALU.add), R=[f"ps{ob}", "rd", "imp"], W=["imp"])
        if C.debug:
            kb.dma("sp", out=C.dbg_y3[0, qtok, :].rearrange("(s p) c -> p s c", p=128), in_=acc[:], R=["acco"], W=["dbg_y3"])
        for sub in range(4):
            kb.op("dve", lambda e: e.tensor_tensor(out=imp2[:], in0=imp[:, sub, :], in1=tkm[:, 0, sub, :], op=ALU.mult), R=["imp", "tkm"], W=["imp2"])
            kb.op("dve", lambda e: e.tensor_tensor(out=imp2[:], in0=imp2[:], in1=tkm[:, 1, sub, :], op=ALU.add), R=["imp2", "tkm"], W=["imp2"])
            kb.op("dve", lambda e: e.max(out=m8[:, 0:8], in_=imp2[:]), R=["imp2"], W=["m8"])
            kb.op("dve", lambda e: e.match_replace(out=tmpk[:], in_to_replace=m8[:, 0:8], in_values=imp2[:], imm_value=-1e30), R=["m8", "imp2"], W=["tmpk"])
            kb.op("dve", lambda e: e.max(out=m8[:, 8:16], in_=tmpk[:]), R=["tmpk"], W=["m8"])
            kb.op("dve", lambda e: e.tensor_scalar(out=m8[:, 15:16], in0=m8[:, 15:16], scalar1=-1e29, scalar2=None, op0=ALU.max), R=["m8"], W=["m8"])
            kb.op("dve", lambda e: e.tensor_scalar(out=selm[:], in0=imp2[:], scalar1=m8[:, 15:16], scalar2=None, op0=ALU.is_ge), R=["imp2", "m8"], W=["selm"])
            kb.op("dve", lambda e: e.tensor_copy(selb[:], selm[:]), R=["selm"], W=["selb"])
            pbs = 0
            pt = C.psb[pbs]
            kb.op("pe", lambda e: e.transpose(pt[0:64, 0:128], selb[:], C.identb[:]), R=["selb", "identb"], W=[f"psb{pbs}"])
            kb.op("dve", lambda e: e.tensor_scalar(out=negT[:, sub * 128:(sub + 1) * 128], in0=pt[0:64, 0:128], scalar1=1.0, scalar2=-NEGM,
                                                   op0=ALU.subtract, op1=ALU.mult), R=[f"psb{pbs}"], W=["negT"])
            if C.debug:
                kb.dma("sp", out=C.dbg_sel[Q * 512 + sub * 128:Q * 512 + (sub + 1) * 128, :], in_=selm[:], R=["selm"], W=["dbg_sel"])
        for branch in ("sel", "win"):
            kT, krn = (kselT, "kselT") if branch == "sel" else (kwinT, "kwinT")
            vA, vrn = (vaugS, "vaugS") if branch == "sel" else (vaugW, "vaugW")
            G, grn = (Gs, "Gs") if branch == "sel" else (Gw, "Gw")
            gi = 1 if branch == "sel" else 2
            kt_lo = 0 if branch == "sel" else max(0, 4 * Q - 4)
            for h in range(4):
                pendq = []
                for kt in range(kt_lo, 4 * Q + 4):
                    b = sc[0] % 3; sc[0] += 1
                    pp, ppn = sbanks[b]
                    ksl = slice(kt * 128, (kt + 1) * 128)
                    if branch == "sel":
                        mm(kb, pp[:, :], kT[:, ksl], qT[:, h, :], True, False, [krn, "qT"], [ppn])
                        mm(kb, pp[:, :], Xbig[:, ksl], negT[:, :], False, True, ["Xbig", "negT"], [ppn])
                    else:
                        mm(kb, pp[:, :], kT[:, ksl], qT[:, h, :], True, True, [krn, "qT"], [ppn])
                    if len(pendq) >= 2:
                        for f_ in pendq.pop(0):
                            f_()
                    pend = []
                    pendq.append(pend)
                    ei = ec[0] % 4; ec[0] += 1
                    ktrel = kt - 4 * Q
                    if branch == "sel" and ktrel <= -2:
                        kb.op("act", lambda e: e.activation(out=eT[ei][:], in_=pp[:, :], func=EXP, bias=cols(C, "relb31", h), scale=0.125),
                              R=[ppn, "colv"], W=[f"eT{ei}"])
                    else:
                        si = ec[0] % 2
                        u0 = 384 - 128 * ktrel
                        kb.op("dve", lambda e: e.scalar_tensor_tensor(out=s2[si][:], in0=pp[:, :], scalar=0.125, in1=G[:, h, u0:u0 + 512],
                                                                      op0=ALU.mult, op1=ALU.add), R=[ppn, grn], W=[f"s2{si}"])
                        kb.op("act", lambda e: e.activation(out=eT[ei][:], in_=s2[si][:], func=EXP), R=[f"s2{si}"], W=[f"eT{ei}"])
                    for sub in range(4):
                        c = 4 * Q + sub
                        lo = 0 if branch == "sel" else max(0, c - 4)
                        if kt < lo or kt > c:
                            continue
                        o0 = (h % 2) * 128

                        def pv(sub=sub, o0=o0, ei=ei, kt=kt, lo=lo, c=c):
                            mm(kb, C.ps[2 + sub][:, o0:o0 + 65], eT[ei][:, sub * 128:(sub + 1) * 128], vA[:, kt, :], kt == lo, kt == c,
                               [f"eT{ei}", vrn], [f"ps{2 + sub}"])
                        pend.append(pv)
                for pl_ in pendq:
                    for f_ in pl_:
                        f_()
                finish_branch(h, gi)
            if C.debug and branch == "sel":
                kb.dma("sp", out=C.dbg_y3[1, qtok, :].rearrange("(s p) c -> p s c", p=128), in_=acc[:], R=["acco"], W=["dbg_y3"])
        for sub in range(4):
            tt = 4 * Q + sub
            tok = slice(tt * 128, (tt + 1) * 128)
            kb.op("dve", lambda e: e.tensor_tensor(out=ma[:], in0=acc[:, sub, :], in1=sza[:, sub, :], op=ALU.mult), R=["acco", "sza"], W=["ma"])
            s_ = tt % 2
            pb = C.psb[0]
            for j in range(2):
                kb.op("pe", lambda e: e.transpose(pb[:, j * 128:(j + 1) * 128], ma[:, j * 128:(j + 1) * 128], C.identb[:]),
                      R=["ma", "identb"], W=["psb0"])
            kb.op("act", lambda e: e.copy(mt[s_][:], pb[:, 0:256].rearrange("p (k t) -> p k t", k=2)), R=["psb0"], W=[f"mtN{s_}"])
            kb.dma("sp", out=mixv[:, 0:2, tok], in_=mt[s_][:], R=[f"mtN{s_}"], W=["mixT_A"])
            if C.debug:
                kb.dma("sp", out=C.dbg_ya[tok, :], in_=acc[:, sub, :], R=["acco"], W=["dbg_ya"])


def phase_R(C, l):
    kb, ar = C.kb, C.arena
    C0 = math.exp(-0.5)
    dve = lambda fn, R, W: kb.op("dve", fn, R=R, W=W)
    act = lambda fn, R, W: kb.op("act", fn, R=R, W=W)
    bank = [0]

    def nb():
        b = bank[0] % 6
        bank[0] += 1
        return b, C.ps[b], f"ps{b}"

    wR = ar.alloc("wR", [128, 8, 832], BF16)
    wZ = ar.alloc("wZ", [128, 8, 256], BF16)
    load_w_bf16(C, wR[:], C.w_in[l][:, O_RW:O_RW + 832], "wR")
    load_w_bf16(C, wZ[:], C.w_in[l][:, O_ZD:O_ZD + 256], "wZ")
    wup = ar.alloc("wup", [32, 2, 256], F32)
    kb.dma("sp", out=wup[:, 0, :], in_=C.rwkv_w_up[l], W=["wup"])
    kb.dma("sp", out=wup[:, 1, :], in_=C.rwkv_a_up[l], W=["wup"])
    rmk = ar.alloc("rmk", [64, 4, 512], F32)
    kb.dma("sp", out=rmk[:], in_=C.rmasks[:, :, :], W=["rmk"])
    MUS, MUI, MLS, IDR = rmk[:, 0, :], rmk[:, 1, :], rmk[:, 2, :], rmk[:, 3, :]
    rst = ar.alloc("rst", [64, 512], F32)
    dve(lambda e: e.memset(rst[:], 1.0), [], ["rst"])
    dve(lambda e: e.memset(rst[:].rearrange("p (a b) -> p a b", b=64)[:, :, 0:1], 0.0), ["rst"], ["rst"])
    ones64 = ar.alloc("ones64", [64, 64], F32)
    dve(lambda e: e.memset(ones64[:], 1.0), [], ["ones64"])
    ident64 = C.cstf[0:64, 0:64]
    carry = ar.alloc("carry", [64, 16], F32)
    dve(lambda e: e.memset(carry[:], 0.0), [], ["carry"])
    ST = [ar.alloc(f"ST{i}", [64, 4, 64], F32) for i in range(2)]
    dve(lambda e: e.memset(ST[0][:], 0.0), [], ["ST0"])
    omka = ar.alloc("omka", [64, 4], F32)
    cv = lambda name: C.colv[0:64, C.col_off[name][0]:C.col_off[name][0] + C.col_off[name][1]]
    dve(lambda e: e.tensor_scalar(out=omka[:], in0=cv("rw_ka"), scalar1=-1.0, scalar2=1.0, op0=ALU.mult, op1=ALU.add), ["colv"], ["omka"])
    f4 = lambda name, dt=F32: ar.alloc(name, [64, 4, 128], dt)
    RW = ar.alloc("RW", [64, 12, 512], F32)
    WD = ar.alloc("WD", [32, 512], F32)
    AD = ar.alloc("AD", [32, 512], F32)
    raw = [ar.alloc(f"raw{i}", [64, 516], F32) for i in range(2)]
    tmpg = [ar.alloc(f"tmpg{i}", [64, 512], F32) for i in range(2)]
    sig, aa, cum, Pinv, Pex = f4("sig"), f4("aa"), f4("cum"), f4("Pinv"), f4("Pex")
    kk, kp, t4 = f4("kk"), f4("kp"), f4("t4")
    AT, BT, KT, BPf, KPf, vb = f4("AT", BF16), f4("BT", BF16), f4("KT", BF16), f4("BPf", BF16), f4("KPf", BF16), f4("vb", BF16)
    p8 = lambda name: ar.alloc(name, [64, 8, 64], BF16)
    A_tm = p8("A_tm")
    X, XT, Xn, XTn, TT = p8("X"), p8("XT"), p8("Xn"), p8("XTn"), p8("TT")
    AKT, Wb = p8("AKT"), p8("Wb")
    D2 = {}
    D2["Pt"] = [f4("Pt" + str(i)) for i in range(2)]
    D2["RT"] = [f4("RT" + str(i), BF16) for i in range(2)]
    for nm_ in ("BP_tm", "KP_tm", "V_tm", "RBT", "RKT", "GT", "Hb"):
        D2[nm_] = [p8(nm_ + str(i)) for i in range(2)]
    D2["rkb"] = [ar.alloc(f"rkb{i}", [64, 8, 2], F32) for i in range(2)]
    Usb = ar.alloc("Usb", [64, 4, 64], BF16)
    STb = [ar.alloc(f"STb{i}", [64, 4, 64], BF16) for i in range(2)]
    dve(lambda e: e.memset(STb[0][:], 0.0), [], ["STb0"])
    identb64 = C.identb[0:64, 0:64]
    Y = ar.alloc("Y", [64, 4, 64], F32)
    Yc = ar.alloc("Yc", [64, 4, 64], F32)
    Ysq = ar.alloc("Ysq", [64, 4, 64], F32)
    stt = ar.alloc("stt", [64, 16], F32)
    szd = ar.alloc("szd", [64, 256], F32)
    md = ar.alloc("md", [64, 256], BF16)
    mtR = [ar.alloc(f"mtR{i}", [128, 2, 64], BF16) for i in range(2)]
    gng, gnb = bcs(C, "gn_g")[0:64, :], bcs(C, "gn_b")[0:64, :]
    mixv = C.mixT.rearrange("k p t -> p k t")
    bcol = lambda name: cv(name).rearrange("p (h o) -> p h o", o=1).to_broadcast([64, 4, 128])
    flat = lambda t: t[:].rearrange("p h t -> p (h t)")
    v8 = lambda t: t[:].rearrange("p h (c t) -> p (h c) t", t=64)
    groups = [(g * 64, 64) for g in range(12)] + [(768, 32), (800, 32)]
    def stage1(tt):
        par = tt % 2
        PAR = str(par)
        tok = slice(tt * 128, (tt + 1) * 128)
        Pt, RT = D2["Pt"][par], D2["RT"][par]
        BP_tm, KP_tm, V_tm, RBT, RKT, GT, Hb, rkb = (D2[k_][par] for k_ in ("BP_tm", "KP_tm", "V_tm", "RBT", "RKT", "GT", "Hb", "rkb"))
        Pend, kka = cum, sig
        sub4 = tt % 4
        if sub4 == 0:
            tok512 = slice(tt * 128, tt * 128 + 512)
            for g, (c0, M) in enumerate(groups):
                b, pp, prn = nb()
                for k in range(8):
                    mm(kb, pp[0:M, 0:512], wR[:, k, c0:c0 + M], C.hT[:, k, tok512], k == 0, k == 7, ["hT", "wR"], [prn])
                rs = g % 2
                rw_, rrn = raw[rs], f"raw{rs}"
                act(lambda e: e.copy(rw_[0:M, 1:513], pp[0:M, 0:512]), [prn], [rrn])
                act(lambda e: e.copy(rw_[0:M, 0:1], carry[0:M, g:g + 1]), ["carry"], [rrn])
                dve(lambda e: e.tensor_tensor(out=tmpg[rs][0:M, :], in0=rw_[0:M, 0:512], in1=rw_[0:M, 1:513], op=ALU.subtract), [rrn], [f"tmpg{rs}"])
                dst = RW[:, g, :] if g < 12 else (WD[:, :] if g == 12 else AD[:, :])
                drn = "RW" if g < 12 else ("WD" if g == 12 else "AD")
                dve(lambda e: e.scalar_tensor_tensor(out=dst, in0=tmpg[rs][0:M, :], scalar=cv("rw_mu")[0:M, g:g + 1], in1=rw_[0:M, 1:513],
                                                     op0=ALU.mult, op1=ALU.add), [f"tmpg{rs}", rrn, "colv"], [drn])
                act(lambda e: e.copy(carry[0:M, g:g + 1], rw_[0:M, 512:513]), [rrn], ["carry"])
                if g % 2 == 1:
                    yield
            act(lambda e: e.activation(out=WD[:, :], in_=WD[:, :], func=AF.Tanh), ["WD"], ["WD"])
        ts4 = slice(sub4 * 128, (sub4 + 1) * 128)
        r_, k_, v_ = RW[:, 0:4, ts4], RW[:, 4:8, ts4], RW[:, 8:12, ts4]
        WDv, ADv = WD[:, ts4], AD[:, ts4]
        yield
        b1, pz, pzn = nb()
        b2, pa, pan = nb()
        for h in range(4):
            mm(kb, pz[0:64, h * 128:(h + 1) * 128], wup[:, 0, h * 64:(h + 1) * 64], WDv, True, True, ["wup", "WD"], [pzn])
        for h in range(4):
            mm(kb, pa[0:64, h * 128:(h + 1) * 128], wup[:, 1, h * 64:(h + 1) * 64], ADv, True, True, ["wup", "AD"], [pan])
        for h in range(4):
            act(lambda e: e.activation(out=sig[:, h, :], in_=pz[0:64, h * 128:(h + 1) * 128], func=AF.Sigmoid, bias=cv("rw_w0")[:, h:h + 1]),
                [pzn, "colv"], ["sig"])
            act(lambda e: e.activation(out=aa[:, h, :], in_=pa[0:64, h * 128:(h + 1) * 128], func=AF.Sigmoid, bias=cv("rw_a0")[:, h:h + 1]),
                [pan, "colv"], ["aa"])
        yield
        dve(lambda e: e.tensor_tensor_scan(out=flat(cum), data0=rst[:], data1=flat(sig), initial=0.0, op0=ALU.mult, op1=ALU.add),
            ["rst", "sig"], ["cum"])
        act(lambda e: e.activation(out=flat(Pt), in_=flat(cum), func=AF.Exp, scale=-C0), ["cum"], ["Pt" + PAR])
        act(lambda e: e.activation(out=flat(Pinv), in_=flat(cum), func=AF.Exp, scale=C0), ["cum"], ["Pinv"])
        dve(lambda e: e.tensor_tensor(out=flat(t4), in0=flat(cum), in1=flat(sig), op=ALU.subtract), ["cum", "sig"], ["t4"])
        act(lambda e: e.activation(out=flat(Pex), in_=flat(t4), func=AF.Exp, scale=-C0), ["t4"], ["Pex"])
        dve(lambda e: e.tensor_tensor(out=v8(Pend), in0=v8(Pinv), in1=v8(Pt)[:, :, 63:64].to_broadcast([64, 8, 64]), op=ALU.mult),
            ["Pinv", "Pt" + PAR], ["cum"])
        yield
        dve(lambda e: e.tensor_tensor(out=kk[:], in0=k_, in1=bcol("rw_kk"), op=ALU.mult), ["RW", "colv"], ["kk"])
        act(lambda e: e.activation(out=flat(t4), in_=flat(kk), func=AF.Square), ["kk"], ["t4"])
        b, pn, pnn = nb()
        mm(kb, pn[0:64, :], ones64[:], flat(t4), True, True, ["ones64", "t4"], [pnn])
        act(lambda e: e.activation(out=flat(t4), in_=pn[0:64, :], func=AF.Sqrt), [pnn], ["t4"])
        dve(lambda e: e.tensor_scalar(out=flat(t4), in0=flat(t4), scalar1=1e-12, scalar2=None, op0=ALU.max), ["t4"], ["t4"])
        dve(lambda e: e.reciprocal(out=flat(t4), in_=flat(t4)), ["t4"], ["t4"])
        dve(lambda e: e.tensor_tensor(out=flat(kk), in0=flat(kk), in1=flat(t4), op=ALU.mult), ["kk", "t4"], ["kk"])
        dve(lambda e: e.tensor_tensor(out=kp[:], in0=aa[:], in1=bcol("rw_ka"), op=ALU.mult), ["aa", "colv"], ["kp"])
        dve(lambda e: e.tensor_tensor(out=kp[:], in0=kp[:], in1=omka[:].rearrange("p (h o) -> p h o", o=1).to_broadcast([64, 4, 128]), op=ALU.add),
            ["kp", "omka"], ["kp"])
        dve(lambda e: e.tensor_tensor(out=kp[:], in0=kp[:], in1=k_, op=ALU.mult), ["kp", "RW"], ["kp"])
        dve(lambda e: e.scalar_tensor_tensor(out=flat(AT), in0=flat(kk), scalar=-1.0, in1=flat(Pex), op0=ALU.mult, op1=ALU.mult), ["kk", "Pex"], ["AT"])
        dve(lambda e: e.tensor_tensor(out=flat(kka), in0=flat(kk), in1=flat(aa), op=ALU.mult), ["kk", "aa"], ["sig"])
        dve(lambda e: e.tensor_tensor(out=flat(BT), in0=flat(kka), in1=flat(Pinv), op=ALU.mult), ["sig", "Pinv"], ["BT"])
        dve(lambda e: e.tensor_tensor(out=flat(KT), in0=flat(kp), in1=flat(Pinv), op=ALU.mult), ["kp", "Pinv"], ["KT"])
        dve(lambda e: e.tensor_tensor(out=RT[:], in0=r_, in1=Pt[:], op=ALU.mult), ["RW", "Pt" + PAR], ["RT" + PAR])
        dve(lambda e: e.tensor_tensor(out=flat(BPf), in0=flat(kka), in1=flat(Pend), op=ALU.mult), ["sig", "cum"], ["BPf"])
        dve(lambda e: e.tensor_tensor(out=flat(KPf), in0=flat(kp), in1=flat(Pend), op=ALU.mult), ["kp", "cum"], ["KPf"])
        dve(lambda e: e.tensor_tensor(out=t4[:], in0=r_, in1=kp[:], op=ALU.mult), ["RW", "kp"], ["t4"])
        dve(lambda e: e.tensor_tensor(out=t4[:], in0=t4[:], in1=bcol("rw_rk"), op=ALU.mult), ["t4", "colv"], ["t4"])
        b, pr, prn_ = nb()
        for c2 in range(2):
            for h in range(4):
                p = c2 * 4 + h
                mm(kb, pr[0:64, p * 2:p * 2 + 2], t4[:, h, c2 * 64:(c2 + 1) * 64], ones64[:, 0:2], True, True, ["t4", "ones64"], [prn_])
        dve(lambda e: e.tensor_copy(rkb[:].rearrange("p a b -> p (a b)"), pr[0:64, 0:16]), [prn_], ["rkb" + PAR])
        yield
        act(lambda e: e.copy(vb[:], v_), ["RW"], ["vb"])
        for qi, (srcf, srn, dstt, drn) in enumerate(((lambda h, c2: AT[:, h, c2 * 64:(c2 + 1) * 64], "AT", A_tm, "A_tm"),
                                                    (lambda h, c2: BPf[:, h, c2 * 64:(c2 + 1) * 64], "BPf", BP_tm, "BP_tm" + PAR),
                                                    (lambda h, c2: KPf[:, h, c2 * 64:(c2 + 1) * 64], "KPf", KP_tm, "KP_tm" + PAR),
                                                    (lambda h, c2: vb[:, h, c2 * 64:(c2 + 1) * 64], "vb", V_tm, "V_tm" + PAR))):
            pt, ptn = C.psb[qi % 2], f"psb{qi % 2}"
            for c2 in range(2):
                for h in range(4):
                    p = c2 * 4 + h
                    kb.op("pe", lambda e: e.transpose(pt[0:64, p * 64:(p + 1) * 64], srcf(h, c2), identb64), R=[srn, "identb"], W=[ptn])
            act(lambda e: e.copy(dstt[:].rearrange("p a b -> p (a b)"), pt[0:64, 0:512]), [ptn], [drn])
            yield
        yield
        fm = lambda t, h, c2: t[:, h, c2 * 64:(c2 + 1) * 64]
        for (lt, ltn, rt_, rtn, msk, dstt, drn) in ((AT, "AT", BT, "BT", MLS, X, "X"), (BT, "BT", AT, "AT", MUS, XT, "XT"),
                                                    (KT, "KT", AT, "AT", MUS, AKT, "AKT"), (BT, "BT", RT, "RT" + PAR, MUI, RBT, "RBT" + PAR),
                                                    (KT, "KT", RT, "RT" + PAR, MUI, RKT, "RKT" + PAR)):
            b, pm, pmn = nb()
            for c2 in range(2):
                for h in range(4):
                    p = c2 * 4 + h
                    mm(kb, pm[0:64, p * 64:(p + 1) * 64], fm(lt, h, c2), fm(rt_, h, c2), True, True, [ltn, rtn], [pmn])
            dve(lambda e: e.tensor_tensor(out=dstt[:].rearrange("p a b -> p (a b)"), in0=pm[0:64, :], in1=msk, op=ALU.mult), [pmn, "rmk"], [drn])
            yield
        f8 = lambda t: t[:].rearrange("p a b -> p (a b)")
        dve(lambda e: e.tensor_tensor(out=f8(TT), in0=f8(XT), in1=IDR, op=ALU.add), ["XT", "rmk"], ["TT"])
        yield
        cx, cxn, cxt, cxtn = X, "X", XT, "XT"
        nx, nxn, nxt_, nxtn = Xn, "Xn", XTn, "XTn"
        for lev in range(1, 6):
            b, p2, p2n = nb()
            for p in range(8):
                mm(kb, p2[0:64, p * 64:(p + 1) * 64], cxt[:, p, :], cx[:, p, :], True, True, [cxn, cxtn], [p2n])
            act(lambda e: e.copy(f8(nx), p2[0:64, :]), [p2n], [nxn])
            yield
            if lev < 5:
                b, p3, p3n = nb()
                for p in range(8):
                    mm(kb, p3[0:64, p * 64:(p + 1) * 64], cx[:, p, :], cxt[:, p, :], True, True, [cxn, cxtn], [p3n])
                act(lambda e: e.copy(f8(nxt_), p3[0:64, :]), [p3n], [nxtn])
            b, p4, p4n = nb()
            for p in range(8):
                mm(kb, p4[0:64, p * 64:(p + 1) * 64], nx[:, p, :], TT[:, p, :], True, True, [nxn, "TT"], [p4n])
            dve(lambda e: e.tensor_tensor(out=f8(TT), in0=f8(TT), in1=p4[0:64, :], op=ALU.add), ["TT", p4n], ["TT"])
            yield
            cx, cxn, cxt, cxtn, nx, nxn, nxt_, nxtn = nx, nxn, nxt_, nxtn, cx, cxn, cxt, cxtn
        yield
        b, pg, pgn = nb()
        for p in range(8):
            mm(kb, pg[0:64, p * 64:(p + 1) * 64], A_tm[:, p, :], TT[:, p, :], True, True, ["A_tm", "TT"], [pgn])
        act(lambda e: e.copy(f8(GT), pg[0:64, :]), [pgn], ["GT" + PAR])
        yield
        b, pw, pwn = nb()
        for p in range(8):
            mm(kb, pw[0:64, p * 64:(p + 1) * 64], AKT[:, p, :], V_tm[:, p, :], True, True, ["AKT", "V_tm" + PAR], [pwn])
        act(lambda e: e.copy(f8(Wb), pw[0:64, :]), [pwn], ["Wb"])
        yield
        b, ph_, phn = nb()
        for p in range(8):
            mm(kb, ph_[0:64, p * 64:(p + 1) * 64], TT[:, p, :], Wb[:, p, :], True, True, ["TT", "Wb"], [phn])
        act(lambda e: e.copy(f8(Hb), ph_[0:64, :]), [phn], ["Hb" + PAR])

    def stage2(tt):
        nonlocal si
        par = tt % 2
        PAR = str(par)
        Pt, RT = D2["Pt"][par], D2["RT"][par]
        BP_tm, KP_tm, V_tm, RBT, RKT, GT, Hb, rkb = (D2[k_][par] for k_ in ("BP_tm", "KP_tm", "V_tm", "RBT", "RKT", "GT", "Hb", "rkb"))
        for c2 in range(2):
            cur, curn, nxs, nxsn = ST[si], f"ST{si}", ST[1 - si], f"ST{1 - si}"
            curb, curbn, nxb, nxbn = STb[si], f"STb{si}", STb[1 - si], f"STb{1 - si}"
            b, pu, pun = nb()
            for h in range(4):
                p = c2 * 4 + h
                mm(kb, pu[0:64, h * 64:(h + 1) * 64], GT[:, p, :], curb[:, h, :], True, False, ["GT" + PAR, curbn], [pun])
                mm(kb, pu[0:64, h * 64:(h + 1) * 64], identb64, Hb[:, p, :], False, True, ["identb", "Hb" + PAR], [pun])
            act(lambda e: e.copy(Usb[:].rearrange("p a b -> p (a b)"), pu[0:64, 0:256]), [pun], ["Usb"])
            yield
            b, pS, pSn = nb()
            for h in range(4):
                p = c2 * 4 + h
                mm(kb, pS[0:64, h * 64:(h + 1) * 64], BP_tm[:, p, :], Usb[:, h, :], True, False, ["BP_tm" + PAR, "Usb"], [pSn])
                mm(kb, pS[0:64, h * 64:(h + 1) * 64], KP_tm[:, p, :], V_tm[:, p, :], False, True, ["KP_tm" + PAR, "V_tm" + PAR], [pSn])
            b, pY, pYn = nb()
            for h in range(4):
                p = c2 * 4 + h
                mm(kb, pY[0:64, h * 64:(h + 1) * 64], RT[:, h, c2 * 64:(c2 + 1) * 64], curb[:, h, :], True, False, ["RT" + PAR, curbn], [pYn])
                mm(kb, pY[0:64, h * 64:(h + 1) * 64], RBT[:, p, :], Usb[:, h, :], False, False, ["RBT" + PAR, "Usb"], [pYn])
                mm(kb, pY[0:64, h * 64:(h + 1) * 64], RKT[:, p, :], V_tm[:, p, :], False, True, ["RKT" + PAR, "V_tm" + PAR], [pYn])
            plb = Pt[:, :, c2 * 64 + 63:c2 * 64 + 64].to_broadcast([64, 4, 64])
            dve(lambda e: e.tensor_tensor(out=nxs[:], in0=cur[:], in1=plb, op=ALU.mult), [curn, "Pt" + PAR], [nxsn])
            dve(lambda e: e.tensor_tensor(out=nxs[:].rearrange("p a b -> p (a b)"), in0=nxs[:].rearrange("p a b -> p (a b)"), in1=pS[0:64, 0:256], op=ALU.add),
                [nxsn, pSn], [nxsn])
            act(lambda e: e.copy(nxb[:], nxs[:]), [nxsn], [nxbn])
            si = 1 - si
            yield
            act(lambda e: e.copy(Y[:].rearrange("p a b -> p (a b)"), pY[0:64, 0:256]), [pYn], ["Y"])
            dve(lambda e: e.reduce_sum(out=stt[:, 0:4], in_=Y[:], axis=AX.X), ["Y"], ["stt"])
            dve(lambda e: e.tensor_scalar(out=stt[:, 4:8], in0=stt[:, 0:4], scalar1=1.0 / 64, scalar2=None, op0=ALU.mult), ["stt"], ["stt"])
            dve(lambda e: e.tensor_tensor(out=Yc[:], in0=Y[:], in1=stt[:, 4:8].rearrange("p (h o) -> p h o", o=1).to_broadcast([64, 4, 64]), op=ALU.subtract),
                ["Y", "stt"], ["Yc"])
            act(lambda e: e.activation(out=Ysq[:].rearrange("p a b -> p (a b)"), in_=Yc[:].rearrange("p a b -> p (a b)"), func=AF.Square), ["Yc"], ["Ysq"])
            dve(lambda e: e.reduce_sum(out=stt[:, 8:12], in_=Ysq[:], axis=AX.X), ["Ysq"], ["stt2"])
            yield
            rsqrt_op(C, stt[:, 12:16], stt[:, 8:12], 1.0 / 64, 64e-5, ["stt2"], ["stt3"])
            dve(lambda e: e.tensor_tensor(out=Yc[:], in0=Yc[:], in1=stt[:, 12:16].rearrange("p (h o) -> p h o", o=1).to_broadcast([64, 4, 64]), op=ALU.mult),
                ["Yc", "stt3"], ["Yc"])
            ycf = Yc[:].rearrange("p a b -> p (a b)")
            dve(lambda e: e.tensor_tensor(out=ycf, in0=ycf, in1=gng, op=ALU.mult), ["Yc", "bcv"], ["Yc"])
            dve(lambda e: e.tensor_tensor(out=ycf, in0=ycf, in1=gnb, op=ALU.add), ["Yc", "bcv"], ["Yc"])
            dve(lambda e: e.tensor_tensor(out=Ysq[:], in0=V_tm[:, c2 * 4:c2 * 4 + 4, :], in1=rkb[:, c2 * 4:c2 * 4 + 4, 0:1].to_broadcast([64, 4, 64]), op=ALU.mult),
                ["V_tm" + PAR, "rkb" + PAR], ["Ysq"])
            dve(lambda e: e.tensor_tensor(out=Yc[:], in0=Yc[:], in1=Ysq[:], op=ALU.add), ["Yc", "Ysq"], ["Yc"])
            yield
            tok64 = slice(tt * 128 + c2 * 64, tt * 128 + (c2 + 1) * 64)
            b, pzd, pzdn = nb()
            for k in range(8):
                mm(kb, pzd[0:64, 0:256], C.hT[:, k, tok64], wZ[:, k, :], k == 0, k == 7, ["hT", "wZ"], [pzdn])
            act(lambda e: e.activation(out=szd[:], in_=pzd[0:64, 0:256], func=AF.Silu), [pzdn], ["szd"])
            yield
            dve(lambda e: e.tensor_tensor(out=md[:], in0=ycf, in1=szd[:], op=ALU.mult), ["Yc", "szd"], ["md"])
            if C.debug:
                kb.dma("sp", out=C.dbg_yd[tok64, :], in_=Y[:].rearrange("p a b -> p (a b)"), R=["Y"], W=["dbg_yd"])
            ms = (tt * 2 + c2) % 2
            pb = C.psb[ms]
            for j in range(2):
                kb.op("pe", lambda e: e.transpose(pb[:, j * 64:(j + 1) * 64], md[:, j * 128:(j + 1) * 128], C.identb[0:64, 0:64]),
                      R=["md", "identb"], W=[f"psb{ms}"])
            act(lambda e: e.copy(mtR[ms][:], pb[:, 0:128].rearrange("p (k t) -> p k t", k=2)), [f"psb{ms}"], [f"mtR{ms}"])
            kb.dma("sp", out=mixv[:, 6:8, tok64], in_=mtR[ms][:], R=[f"mtR{ms}"], W=["mixT_D"])
            yield

    si = 0

    def run_interleaved(ga, gb, ratio=4):
        a_live, b_live = ga is not None, gb is not None
        while a_live or b_live:
            for _ in range(ratio):
                if a_live:
                    try:
                        next(ga)
                    except StopIteration:
                        a_live = False
            if b_live:
                try:
                    next(gb)
                except StopIteration:
                    b_live = False

    run_interleaved(stage1(0), None)
    for tt in range(32):
        run_interleaved(stage1(tt + 1) if tt + 1 < 32 else None, stage2(tt))


def host_prep(inp):
    f = lambda k: np.asarray(inp[k], dtype=np.float32)
    bcs_, cols_, gpp_ = [], [], []
    bc_off = col_off = None
    for l in range(NL):
        b = Pack(128)
        b.add("sgu_ln_g", bc_rows(f("sgu_ln_g")[l]))
        b.add("sgu_ln_b", bc_rows(f("sgu_ln_b")[l]))
        b.add("gn_g", bc_rows(f("rwkv_gn_g")[l]))
        b.add("gn_b", bc_rows(f("rwkv_gn_b")[l]))
        c = Pack(128)
        c.add("sgu_bT", f("sgu_b")[l].T)
        cw = f("conv_w")[l]
        c.add("conv_wT", np.concatenate([cw[:, j * 128:(j + 1) * 128].T for j in range(2)], axis=1))
        for nm in ("conv_b", "conv_ln_g", "conv_ln_b", "conv_pw_b"):
            c.add(nm, col_chunks(f(nm)[l]))
        c.add("relb31", bc_rows(f("rel_bias")[31]))
        pad64 = lambda a: np.concatenate([a, np.zeros((128 - a.shape[0], a.shape[1]), np.float32)], axis=0)
        mu = f("rwkv_mu")[l]
        mu_cols = np.zeros((64, 14), np.float32)
        mu_cols[:, 0:12] = mu[0:768].reshape(12, 64).T
        mu_cols[0:32, 12] = mu[768:800]
        mu_cols[0:32, 13] = mu[800:832]
        c.add("rw_mu", pad64(mu_cols))
        for nm, key in (("rw_w0", "rwkv_w0"), ("rw_a0", "rwkv_a0"), ("rw_kk", "rwkv_k_k"), ("rw_ka", "rwkv_k_a")):
            c.add(nm, pad64(f(key)[l].reshape(4, 64).T))
        c.add("rw_rk", pad64(f("rwkv_r_k")[l].T))
        gpp_.append(np.concatenate([bc_rows(f("g_pre")[l]), bc_rows(f("g_post")[l])], axis=1))
        bcs_.append(b.build()); cols_.append(c.build())
        bc_off, col_off = b.off, c.off
    cst = np.zeros((128, 1024), np.float32)
    cst[:, 0:128] = np.eye(128, dtype=np.float32)
    ii = np.arange(128)
    cst[:, 128:256] = (ii[None, :] >= ii[:, None]).astype(np.float32)
    cst[:, 256:384] = 1.0 / 256.0
    rb = f("rel_bias")
    tpos = np.arange(S)
    npos = np.arange(256)
    dc = tpos[None, :] - (16 * npos[:, None] + 31)
    cb = np.where((dc >= 0)[None], rb[rel_bucket_np(dc)].transpose(2, 0, 1), np.float32(NEGM)).astype(np.float32)
    cb[:, 255, :] = NEGM
    def strip(width, lim):
        dd = np.arange(width)[None, :] - 384 - np.arange(128)[:, None]
        ok = (dd >= 0) if lim is None else ((dd >= 0) & (dd < lim))
        v = rb[rel_bucket_np(dd)]
        return np.ascontiguousarray(np.where(ok[:, :, None], v, np.float32(NEGM)).transpose(0, 2, 1).astype(np.float32))
    gs_, gw_ = strip(1024, None), strip(1408, 512)
    xbig = np.zeros((64, S), np.float32)
    xbig[np.arange(S) // 64, np.arange(S)] = 1.0
    mmat = np.zeros((256, 64), np.float32)
    for j in range(64):
        for n in range(4 * j - 1, 4 * j + 4):
            if 0 <= n < 255:
                mmat[n, j] = 1.0
    tk_mul = np.zeros((32, 128, 64), np.float32)
    tk_add = np.zeros((32, 128, 64), np.float32)
    tt_ = np.arange(S).reshape(32, 128)
    cur = tt_ // 64
    jj = np.arange(64)[None, None, :]
    causal = jj <= cur[:, :, None]
    f0 = (jj == 0) & causal
    f1 = (jj == cur[:, :, None])
    f2 = (jj == cur[:, :, None] - 1)
    forced = f0 | f1 | f2
    tk_mul[causal & ~forced] = 1.0
    tk_add[~causal] = -1e30
    tk_add[f0] = 10000.0
    tk_add[f1] = 10001.0
    tk_add[f2] = 10002.0
    i64 = np.arange(64)
    mus = (i64[None, :] > i64[:, None]).astype(np.float32)
    mui = (i64[None, :] >= i64[:, None]).astype(np.float32)
    mls = (i64[None, :] < i64[:, None]).astype(np.float32)
    rmasks = np.stack([np.tile(m_, (1, 8)) for m_ in (mus, mui, mls, np.eye(64, dtype=np.float32))], axis=1)
    shared = {
        "rwkv_w_up": f("rwkv_w_up"), "rwkv_a_up": f("rwkv_a_up"), "rmasks": np.ascontiguousarray(rmasks),
        "nsa_w1": f("nsa_w1"), "nsa_w2": f("nsa_w2"),
        "nsa_posT": np.ascontiguousarray(f("nsa_pos").transpose(0, 1, 3, 2)),
        "xbig": xbig, "gs": gs_, "gw": gw_, "mmat": mmat, "tk_mul": tk_mul, "tk_add": tk_add, "cbias": cb,
        "w_in": f("w_in"), "w_out": f("w_out"), "ple_gate": f("ple_gate"), "ple_proj": f("ple_proj"),
        "bc": np.stack(bcs_), "col": np.stack(cols_), "cst": cst, "gpp": np.ascontiguousarray(np.stack(gpp_)),
        "sgu_wT": np.ascontiguousarray(f("sgu_w").transpose(0, 3, 1, 2)),
        "conv_pw": f("conv_pw"),
    }
    return shared, bc_off, col_off


_CACHE = {}
LAUNCH_GROUPS = ((0, 1),)


def kernel(**inputs):
    shared, bc_off, col_off = host_prep(inputs)
    n_bc, n_col = shared["bc"].shape[2], shared["col"].shape[2]
    x = np.asarray(inputs["x"], dtype=np.float32)
    p = np.asarray(inputs["p"], dtype=np.float32)
    cur = [np.ascontiguousarray(x[b]) for b in range(4)]
    for grp in LAUNCH_GROUPS:
        key = (n_bc, n_col, grp)
        if key not in _CACHE:
            _CACHE[key] = build_program(bc_off, col_off, n_bc, n_col, layers=grp)
        nc, kb = _CACHE[key]
        in_maps = []
        for c in range(8):
            b = c % 4
            m = dict(shared)
            m["x"] = cur[b]
            m["p"] = np.ascontiguousarray(p[:, b])
            in_maps.append(m)
        res = run_bass_kernel_spmd(nc, in_maps, core_ids=list(range(8)))
        cur = [np.ascontiguousarray(np.asarray(res.results[b]["out"], dtype=np.float32)) for b in range(4)]
    return np.stack(cur, axis=0)
```

```python
import math
import numpy as np
import ml_dtypes
import concourse.bass as bass
import concourse.mybir as mybir
from concourse.bass_utils import run_bass_kernel_spmd

F32 = mybir.dt.float32
BF16 = mybir.dt.bfloat16
AF = mybir.ActivationFunctionType
ALU = mybir.AluOpType
AX = mybir.AxisListType

S = 4096
D = 1024
NL = 2
N_IN = 3532
O_Q, O_KC, O_VC, O_KS, O_VS, O_KW, O_VW, O_G, O_ZA = 0, 256, 320, 384, 448, 512, 576, 640, 652
O_U, O_V, O_ZB = 908, 1164, 1420
O_GA, O_GB, O_ZC = 1676, 1932, 2188
O_RW, O_ZD = 2444, 3276
NEGM = -30000.0


class KB:
    COMPUTE = ("pe", "act", "dve", "pool")
    EPOCH = 12000

    def __init__(self, nc, n_dma_sems=24):
        self.nc = nc
        self.E = {"pe": nc.tensor, "act": nc.scalar, "dve": nc.vector, "pool": nc.gpsimd, "sp": nc.sync}
        self.csem = {e: nc.alloc_semaphore(f"c_{e}_0") for e in self.COMPUTE}
        self.cepoch = {e: 0 for e in self.COMPUTE}
        self.ccnt = {e: 0 for e in self.COMPUTE}
        self.waited = {e: {} for e in self.E}
        self.dsems = [nc.alloc_semaphore(f"dq{i}") for i in range(n_dma_sems)]
        self.dcnt = [0] * n_dma_sems
        self.dnext = 0
        self.lastw = {}
        self.readers = {}
        self.semobj = {}
        self.n_ins = 0
        self.all_tokens = {}
        self.relax = False

    def _wait(self, e, tok):
        key, val, src = tok
        if src == "pe" and e == "pe":
            return
        if self.relax and src == e:
            return
        w = self.waited[e]
        if w.get(key, 0) >= val:
            return
        self.E[e].wait_ge(self.semobj[key], val)
        w[key] = val
        self.n_ins += 1

    def _deps(self, e, R, W):
        for r in R:
            t = self.lastw.get(r)
            if t is not None:
                self._wait(e, t)
        for w_ in W:
            t = self.lastw.get(w_)
            if t is not None:
                self._wait(e, t)
            for t in self.readers.get(w_, {}).values():
                self._wait(e, t)

    def _record(self, tok, R, W):
        for r in R:
            self.readers.setdefault(r, {})[tok[0]] = tok
        for w_ in W:
            self.lastw[w_] = tok
            self.readers[w_] = {}
        self.all_tokens[tok[0]] = tok

    def op(self, e, ins_fn, R=(), W=(), relax=False):
        self.relax = relax
        self._deps(e, R, W)
        self.relax = False
        if self.ccnt[e] >= self.EPOCH:
            self.cepoch[e] += 1
            self.csem[e] = self.nc.alloc_semaphore(f"c_{e}_{self.cepoch[e]}")
            self.ccnt[e] = 0
        ins = ins_fn(self.E[e])
        self.ccnt[e] += 1
        key = f"c_{e}_{self.cepoch[e]}"
        self.semobj[key] = self.csem[e]
        ins.then_inc(self.csem[e], 1)
        tok = (key, self.ccnt[e], e)
        self._record(tok, R, W)
        self.n_ins += 1
        return tok

    def dma(self, q, out, in_, R=(), W=(), **kw):
        self._deps(q, R, W)
        if q == "pool":
            self.nsw = getattr(self, "nsw", 0) + 1
            key = f"dsw{self.nsw}"
            sem = self.nc.alloc_semaphore(key)
            self.semobj[key] = sem
            self.E[q].dma_start(out=out, in_=in_, **kw).then_inc(sem, 16)
            tok = (key, 16, "dma")
            self._record(tok, R, W)
            self.n_ins += 1
            return tok
        i = self.dnext
        self.dnext = (self.dnext + 1) % len(self.dsems)
        key = f"dq{i}"
        self.semobj[key] = self.dsems[i]
        if self.dcnt[i] > 0:
            self._wait(q, (key, 16 * self.dcnt[i], "dma"))
        self.E[q].dma_start(out=out, in_=in_, **kw).then_inc(self.dsems[i], 16)
        self.dcnt[i] += 1
        tok = (key, 16 * self.dcnt[i], "dma")
        self._record(tok, R, W)
        self.n_ins += 1
        return tok

    def barrier(self):
        toks = list(self.all_tokens.values())
        for e in self.E:
            for t in toks:
                if t[2] == e:
                    continue
                self._wait(e, t)
        for e in self.COMPUTE:
            if e == "pe":
                continue
            for t in toks:
                if t[2] == e:
                    self._wait(e, t)
        self.lastw = {}
        self.readers = {}

    def finish(self):
        toks = list(self.all_tokens.values())
        for t in toks:
            self._wait("sp", t)


class Arena:
    def __init__(self, nc, base, limit):
        self.nc, self.base, self.limit = nc, base, limit
        self.off = base
        self.uid = 0

    def reset(self, to=None):
        self.off = self.base if to is None else to

    def mark(self):
        return self.off

    def alloc(self, name, shape, dtype):
        nbytes = int(np.prod(shape[1:])) * (4 if dtype == F32 else 2)
        nbytes = (nbytes + 31) // 32 * 32
        assert self.off + nbytes <= self.limit, f"SBUF arena overflow for {name}: {self.off}+{nbytes}>{self.limit}"
        self.uid += 1
        t = self.nc.alloc_sbuf_tensor_at(f"{name}_{self.uid}", list(shape), dtype, offset=self.off)
        self.off += nbytes
        return t


class Pack:
    def __init__(self, rows):
        self.rows = rows
        self.parts = []
        self.off = {}
        self.n = 0

    def add(self, name, arr):
        arr = np.ascontiguousarray(arr, dtype=np.float32)
        assert arr.shape[0] == self.rows, (name, arr.shape)
        arr = arr.reshape(self.rows, -1)
        self.off[name] = (self.n, arr.shape[1])
        self.parts.append(arr)
        self.n += arr.shape[1]

    def build(self):
        return np.concatenate(self.parts, axis=1)


def bc_rows(v, rows=128):
    v = np.asarray(v, dtype=np.float32).reshape(1, -1)
    return np.broadcast_to(v, (rows, v.shape[1]))


def col_chunks(v):
    v = np.asarray(v, dtype=np.float32)
    return v.reshape(-1, 128).T


def rel_bucket_np(dist):
    n = np.maximum(dist, 0)
    nf = np.maximum(n, 16).astype(np.float32)
    large = 16 + (np.log(nf / np.float32(16)) / np.float32(math.log(128 / 16)) * np.float32(16)).astype(np.int32)
    large = np.minimum(large, 31)
    return np.where(n < 16, n, large)


class Ctx:
    pass


def mm(kb, out, lhsT, rhs, start, stop, R, W):
    kb.op("pe", lambda e: e.matmul(out, lhsT, rhs, start=start, stop=stop), R=R, W=W)


def build_program(bc_off, col_off, n_bc, n_col, phases=("A", "N", "B", "C", "R", "E"), debug=False, nlayers=NL, layers=None):
    nc = bass.Bass("TRN2", target_bir_lowering=False)
    kb = KB(nc)
    C = Ctx()
    C.nc, C.kb = nc, kb
    C.bc_off, C.col_off = bc_off, col_off
    C.debug = debug
    din = lambda name, shape, dt=F32: nc.dram_tensor(name, list(shape), dt, kind="ExternalInput").ap()
    C.x_in = din("x", [S, D])
    C.p_in = din("p", [NL, S, 256])
    C.w_in = din("w_in", [NL, D, N_IN])
    C.w_out = din("w_out", [NL, D, D])
    C.ple_gate = din("ple_gate", [NL, D, D])
    C.ple_proj = din("ple_proj", [NL, 256, D])
    C.bc = din("bc", [NL, 128, n_bc])
    C.col = din("col", [NL, 128, n_col])
    C.gpp = din("gpp", [NL, 128, 2 * D])
    C.cst = din("cst", [128, 1024])
    C.sgu_wT = din("sgu_wT", [NL, 128, 4, 128])
    C.conv_pw = din("conv_pw", [NL, 256, 256])
    C.nsa_w1 = din("nsa_w1", [NL, 2, 2048, 128])
    C.nsa_w2 = din("nsa_w2", [NL, 2, 128, 64])
    C.nsa_posT = din("nsa_posT", [NL, 2, 64, 32])
    C.xbig = din("xbig", [64, S])
    C.gs = din("gs", [128, 4, 1024])
    C.gw = din("gw", [128, 4, 1408])
    C.mmat = din("mmat", [256, 64])
    C.tk_mul = din("tk_mul", [32, 128, 64])
    C.tk_add = din("tk_add", [32, 128, 64])
    C.cbias = din("cbias", [4, 256, S])
    C.rwkv_w_up = din("rwkv_w_up", [NL, 32, 256])
    C.rwkv_a_up = din("rwkv_a_up", [NL, 32, 256])
    C.rmasks = din("rmasks", [64, 4, 512])
    if debug:
        C.dbg_yd = nc.dram_tensor("dbg_yd", [S, 256], F32, kind="ExternalOutput").ap()
        C.dbg_sel = nc.dram_tensor("dbg_sel", [S, 64], F32, kind="ExternalOutput").ap()
        C.dbg_ya = nc.dram_tensor("dbg_ya", [S, 256], F32, kind="ExternalOutput").ap()
        C.dbg_y3 = nc.dram_tensor("dbg_y3", [2, S, 256], F32, kind="ExternalOutput").ap()
    C.out = nc.dram_tensor("out", [S, D], F32, kind="ExternalOutput").ap()
    C.xmid = nc.dram_tensor("xmid", [S, D], F32).ap()
    mk = "ExternalOutput" if debug else "Internal"
    C.mixT = nc.dram_tensor("mixT", [8, 128, S], BF16, kind=mk).ap()

    st = Arena(nc, 16640, 229376)
    C.hT = st.alloc("hT", [128, 8, S], BF16)
    C.identb = st.alloc("identb", [128, 128], BF16)
    C.cstf = st.alloc("cstf", [128, 1024], F32)
    C.bcv = st.alloc("bcv", [128, n_bc], F32)
    C.colv = st.alloc("colv", [128, n_col], F32)
    C.arena = Arena(nc, st.off, 229376)
    C.ps = [nc.alloc_psum_tensor(f"ps{i}", [128, 512], F32) for i in range(6)]
    C.psb = [nc.alloc_psum_tensor(f"psb{i}", [128, 1024], BF16) for i in range(2)]

    kb.dma("sp", out=C.cstf[:], in_=C.cst[:, :], W=["cstf"])
    kb.op("dve", lambda e: e.tensor_copy(C.identb[:], C.cstf[:, 0:128]), R=["cstf"], W=["identb"])

    if layers is None:
        layers = tuple(range(nlayers))
    for li, l in enumerate(layers):
        xsrc = C.x_in if li == 0 else C.xmid
        xdst = C.out if li == len(layers) - 1 else C.xmid
        kb.dma("sp", out=C.bcv[:], in_=C.bc[l], W=["bcv"])
        kb.dma("sp", out=C.colv[:], in_=C.col[l], W=["colv"])
        if "A" in phases:
            phase_ABC(C, l, xsrc)
        if "N" in phases:
            kb.barrier(); C.arena.reset()
            phase_N(C, l)
        if "R" in phases:
            kb.barrier(); C.arena.reset()
            phase_R(C, l)
        if "E" in phases:
            kb.barrier(); C.arena.reset()
            phase_E(C, l, xsrc, xdst)
        kb.barrier(); C.arena.reset()
    kb.finish()
    return nc, kb


def rsqrt_op(C, out, in_, scale, eps, R, W):
    kb = C.kb
    kb.op("dve", lambda e: e.tensor_scalar(out=out, in0=in_, scalar1=scale, scalar2=eps, op0=ALU.mult, op1=ALU.add), R=R, W=W)
    kb.op("act", lambda e: e.activation(out=out, in_=out, func=AF.Sqrt), R=W, W=W)
    kb.op("dve", lambda e: e.reciprocal(out=out, in_=out), R=W, W=W)


def bcs(C, name):
    o, n = C.bc_off[name]
    return C.bcv[:, o:o + n]


def cols(C, name, j=0, n=1):
    o, _ = C.col_off[name]
    return C.colv[:, o + j:o + j + n]


def load_w_bf16(C, dst, src_ap, res, q="pool"):
    C.kb.dma(q, out=dst, in_=src_ap.rearrange("(k p) c -> p k c", p=128), W=[res])


def gelu_tanh(C, out, src_ps, t1, sg, R, W, n):
    kb = C.kb
    kb.op("act", lambda e: e.activation(out=t1, in_=src_ps, func=AF.Square), R=R, W=[n + "t1"])
    kb.op("dve", lambda e: e.tensor_scalar(out=t1, in0=t1, scalar1=0.044715, scalar2=1.0, op0=ALU.mult, op1=ALU.add),
          R=[n + "t1"], W=[n + "t1"])
    kb.op("dve", lambda e: e.tensor_tensor(out=t1, in0=src_ps, in1=t1, op=ALU.mult), R=R + [n + "t1"], W=[n + "t1"])
    kb.op("act", lambda e: e.activation(out=sg, in_=t1, func=AF.Sigmoid, scale=1.5957691216057308), R=[n + "t1"], W=[n + "sg"])
    kb.op("dve", lambda e: e.tensor_tensor(out=out, in0=src_ps, in1=sg, op=ALU.mult), R=R + [n + "sg"], W=W)


def gen_A(C, l, xsrc, prog):
    kb = C.kb
    ar = C.arena
    sq = ar.alloc("sq", [128, D], F32)
    hb = [ar.alloc(f"hb{i}", [128, D], BF16) for i in range(2)]
    ss = ar.alloc("ss", [128, 4], F32)
    gpre_t = ar.alloc("gpre", [128, D], F32)
    kb.dma("sp", out=gpre_t[:], in_=C.gpp[l][:, 0:D], W=["bcv"])
    gpre = gpre_t[:]
    xins = [ar.alloc(f"xin{i}", [128, D], F32) for i in range(2)]
    for tt in range(32):
        s_ = tt % 2
        xin = xins[s_]
        tok = slice(tt * 128, (tt + 1) * 128)
        kb.dma("sp", out=xin[:], in_=xsrc[tok, :], W=[f"xin{s_}"])
        kb.op("act", lambda e: e.activation(out=sq[:], in_=xin[:], func=AF.Square), R=[f"xin{s_}"], W=["sq"])
        kb.op("dve", lambda e: e.reduce_sum(out=ss[:, s_:s_ + 1], in_=sq[:], axis=AX.X), R=["sq"], W=[f"ss{s_}"])
        rsqrt_op(C, ss[:, 2 + s_:3 + s_], ss[:, s_:s_ + 1], 1.0 / D, 1e-6, [f"ss{s_}"], [f"rs{s_}"])
        kb.op("dve", lambda e: e.scalar_tensor_tensor(out=hb[s_][:], in0=xin[:], scalar=ss[:, 2 + s_:3 + s_], in1=gpre,
                                                      op0=ALU.mult, op1=ALU.mult),
              R=[f"xin{s_}", f"rs{s_}", "bcv"], W=[f"hb{s_}"])
        pb = C.psb[0]
        for k in range(8):
            kb.op("pe", lambda e: e.transpose(pb[:, k * 128:(k + 1) * 128], hb[s_][:, k * 128:(k + 1) * 128], C.identb[:]),
                  R=[f"hb{s_}", "identb"], W=["psb0"])
        kb.op("act", lambda e: e.copy(C.hT[:, :, tok], pb[:, :].rearrange("p (k t) -> p k t", k=8)),
              R=["psb0"], W=[f"hT{tt}"])
        prog[0] = tt + 1
        yield


def gen_B(C, l, prog):
    kb, ar = C.kb, C.arena
    wB = ar.alloc("wB", [128, 8, 768], BF16)
    load_w_bf16(C, wB[:], C.w_in[l][:, O_U:O_U + 768], "wB")
    wsf = ar.alloc("wsf", [128, 4, 128], F32)
    wsb = ar.alloc("wsb", [128, 4, 128], BF16)
    kb.dma("sp", out=wsf[:], in_=C.sgu_wT[l], W=["wsf"])
    trilT = C.cstf[:, 128:256]
    for h in range(4):
        kb.op("dve", lambda e: e.tensor_tensor(out=wsb[:, h, :], in0=wsf[:, h, :], in1=trilT, op=ALU.mult),
              R=["wsf", "cstf"], W=["wsb"])
    t1 = ar.alloc("t1", [128, 512], F32)
    sg = ar.alloc("sg", [128, 512], F32)
    guv = ar.alloc("guv", [128, 512], F32)
    st6 = ar.alloc("st6", [128, 6], F32)
    mv = ar.alloc("mv", [128, 4], F32)
    vn = ar.alloc("vn", [128, 256], F32)
    vnb = ar.alloc("vnb", [128, 256], BF16)
    yb = ar.alloc("yb", [128, 256], F32)
    sz = ar.alloc("sz", [128, 256], F32)
    mb = ar.alloc("mb", [128, 256], BF16)
    mt = [ar.alloc(f"mt{i}", [128, 2, 128], BF16) for i in range(2)]
    lng, lnb = bcs(C, "sgu_ln_g"), bcs(C, "sgu_ln_b")
    mixv = C.mixT.rearrange("k p t -> p k t")
    for tt in range(32):
        while prog[0] < tt + 1:
            yield
        tok = slice(tt * 128, (tt + 1) * 128)
        pu, pz, pv = C.ps[0], C.ps[1], C.ps[2]
        for k in range(8):
            mm(kb, pu[:, :], C.hT[:, k, tok], wB[:, k, 0:512], k == 0, k == 7, [f"hT{tt}", "wB"], ["ps0"])
        for k in range(8):
            mm(kb, pz[:, 0:256], C.hT[:, k, tok], wB[:, k, 512:768], k == 0, k == 7, [f"hT{tt}", "wB"], ["ps1"])
        yield
        gelu_tanh(C, guv[:], pu[:, :], t1[:], sg[:], ["ps0"], ["guv"], "B")
        yield
        kb.op("dve", lambda e: e.bn_stats(out=st6[:], in_=guv[:, 256:512]), R=["guv"], W=["st6"])
        kb.op("dve", lambda e: e.bn_aggr(out=mv[:, 0:2], in_=st6[:]), R=["st6"], W=["mv"])
        rsqrt_op(C, mv[:, 2:3], mv[:, 1:2], 1.0, 1e-5, ["mv"], ["mvr"])
        kb.op("dve", lambda e: e.tensor_scalar(out=vn[:], in0=guv[:, 256:512], scalar1=mv[:, 0:1], scalar2=mv[:, 2:3],
                                               op0=ALU.subtract, op1=ALU.mult), R=["guv", "mv", "mvr"], W=["vn"])
        kb.op("dve", lambda e: e.tensor_tensor(out=vn[:], in0=vn[:], in1=lng, op=ALU.mult), R=["vn", "bcv"], W=["vn"])
        kb.op("dve", lambda e: e.tensor_tensor(out=vnb[:], in0=vn[:], in1=lnb, op=ALU.add), R=["vn", "bcv"], W=["vnb"])
        yield
        for h in range(4):
            hs = slice(h * 64, (h + 1) * 64)
            mm(kb, pv[:, hs], wsb[:, h, :], vnb[:, hs], True, True, ["wsb", "vnb"], ["ps2"])
        for h in range(4):
            hs = slice(h * 64, (h + 1) * 64)
            kb.op("dve", lambda e: e.scalar_tensor_tensor(out=yb[:, hs], in0=pv[:, hs], scalar=cols(C, "sgu_bT", h),
                                                          in1=guv[:, hs], op0=ALU.add, op1=ALU.mult),
                  R=["ps2", "colv", "guv"], W=["yb"])
        kb.op("act", lambda e: e.activation(out=sz[:], in_=pz[:, 0:256], func=AF.Silu), R=["ps1"], W=["sz"])
        kb.op("dve", lambda e: e.tensor_tensor(out=mb[:], in0=yb[:], in1=sz[:], op=ALU.mult), R=["yb", "sz"], W=["mb"])
        yield
        s_ = tt % 2
        pb = C.psb[1]
        for j in range(2):
            kb.op("pe", lambda e: e.transpose(pb[:, j * 128:(j + 1) * 128], mb[:, j * 128:(j + 1) * 128], C.identb[:]),
                  R=["mb", "identb"], W=["psb1"])
        kb.op("act", lambda e: e.copy(mt[s_][:], pb[:, 0:256].rearrange("p (k t) -> p k t", k=2)),
              R=["psb1"], W=[f"mt{s_}"])
        kb.dma("sp", out=mixv[:, 2:4, tok], in_=mt[s_][:], R=[f"mt{s_}"], W=["mixT_B"])
        yield


def gen_C(C, l, prog):
    kb, ar = C.kb, C.arena
    wC = ar.alloc("wC", [128, 8, 768], BF16)
    load_w_bf16(C, wC[:], C.w_in[l][:, O_GA:O_GA + 768], "wC")
    wpw = ar.alloc("wpw", [128, 2, 256], BF16)
    load_w_bf16(C, wpw[:], C.conv_pw[l], "wpw")
    xg = [ar.alloc(f"xg{j}", [128, 544], F32) for j in range(2)]
    acc = [ar.alloc(f"acc{j}", [128, 512], F32) for j in range(2)]
    acc2 = [ar.alloc(f"accp{j}", [128, 512], F32) for j in range(2)]
    tmpA = [ar.alloc(f"tmpA{j}", [128, 512], F32) for j in range(2)]
    sqc = [ar.alloc(f"sqc{j}", [128, 512], F32) for j in range(2)]
    szc = [ar.alloc(f"szc{j}", [128, 512], F32) for j in range(2)]
    yn = [ar.alloc(f"yn{j}", [128, 512], BF16) for j in range(2)]
    sig = ar.alloc("sig", [128, 512], F32)
    m2 = ar.alloc("m2", [128, 512], F32)
    rstd = ar.alloc("rstd", [128, 512], F32)
    tmp = ar.alloc("tmpc", [128, 512], F32)
    mc = [ar.alloc(f"mc{j}", [128, 512], BF16) for j in range(2)]
    onesm = C.cstf[:, 256:384]
    for j in range(2):
        kb.op("dve", lambda e: e.memset(xg[j][:, 0:30], 0.0), W=[f"xg{j}"])
    for T in range(8):
        while prog[0] < 4 * T + 4:
            yield
        tok = slice(T * 512, (T + 1) * 512)
        hres = [f"hT{4 * T + i_}" for i_ in range(4)]
        for j in range(2):
            pa, pb_, pz = C.ps[3], C.ps[4], C.ps[5]
            for (pp, c0, rn) in ((pa, j * 128, "ps3"), (pb_, 256 + j * 128, "ps4"), (pz, 512 + j * 128, "ps5")):
                for k in range(8):
                    mm(kb, pp[:, :], wC[:, k, c0:c0 + 128], C.hT[:, k, tok], k == 0, k == 7, hres + ["wC"], [rn])
            yield
            kb.op("act", lambda e: e.activation(out=sig[:], in_=pb_[:, :], func=AF.Sigmoid), R=["ps4"], W=["sig"])
            kb.op("dve", lambda e: e.tensor_tensor(out=xg[j][:, 30:542], in0=pa[:, :], in1=sig[:], op=ALU.mult),
                  R=["ps3", "sig"], W=[f"xg{j}"])
            kb.op("act", lambda e: e.activation(out=szc[j][:], in_=pz[:, :], func=AF.Silu), R=["ps5"], W=[f"szc{j}"])
            kb.op("dve", lambda e: e.tensor_scalar(out=acc[j][:], in0=xg[j][:, 0:512], scalar1=cols(C, "conv_wT", j * 31),
                                                   scalar2=cols(C, "conv_b", j), op0=ALU.mult, op1=ALU.add),
                  R=[f"xg{j}", "colv"], W=[f"acc{j}"])
            for jj in range(1, 31):
                kb.op("dve", lambda e: e.scalar_tensor_tensor(out=acc[j][:], in0=xg[j][:, jj:jj + 512],
                                                              scalar=cols(C, "conv_wT", j * 31 + jj), in1=acc[j][:],
                                                              op0=ALU.mult, op1=ALU.add),
                      R=[f"xg{j}", "colv", f"acc{j}"], W=[f"acc{j}"])
                if jj % 4 == 0:
                    yield
            kb.op("dve", lambda e: e.tensor_copy(xg[j][:, 0:30], xg[j][:, 512:542]), R=[f"xg{j}"], W=[f"xg{j}"])
            kb.op("act", lambda e: e.activation(out=sqc[j][:], in_=acc[j][:], func=AF.Square), R=[f"acc{j}"], W=[f"sqc{j}"])
        yield
        pm, pq = C.ps[3], C.ps[4]
        for j in range(2):
            mm(kb, pm[:, :], onesm, acc[j][:], j == 0, j == 1, ["cstf", f"acc{j}"], ["ps3"])
        for j in range(2):
            mm(kb, pq[:, :], onesm, sqc[j][:], j == 0, j == 1, ["cstf", f"sqc{j}"], ["ps4"])
        kb.op("act", lambda e: e.activation(out=m2[:], in_=pm[:, :], func=AF.Square), R=["ps3"], W=["m2"])
        kb.op("dve", lambda e: e.tensor_tensor(out=rstd[:], in0=pq[:, :], in1=m2[:], op=ALU.subtract), R=["ps4", "m2"], W=["rstd"])
        rsqrt_op(C, rstd[:], rstd[:], 1.0, 1e-5, ["rstd"], ["rstd"])
        for j in range(2):
            kb.op("dve", lambda e: e.tensor_tensor(out=tmp[:], in0=acc[j][:], in1=pm[:, :], op=ALU.subtract),
                  R=[f"acc{j}", "ps3"], W=["tmpc"])
            kb.op("dve", lambda e: e.tensor_tensor(out=tmp[:], in0=tmp[:], in1=rstd[:], op=ALU.mult), R=["tmpc", "rstd"], W=["tmpc"])
            kb.op("act", lambda e: e.activation(out=yn[j][:], in_=tmp[:], func=AF.Silu, bias=cols(C, "conv_ln_b", j),
                                                scale=cols(C, "conv_ln_g", j)), R=["tmpc", "colv"], W=[f"yn{j}"])
        yield
        mixv = C.mixT
        for jo in range(2):
            po = C.ps[5]
            for j in range(2):
                mm(kb, po[:, :], wpw[:, j, jo * 128:(jo + 1) * 128], yn[j][:], j == 0, j == 1, ["wpw", f"yn{j}"], ["ps5"])
            kb.op("dve", lambda e: e.scalar_tensor_tensor(out=mc[jo][:], in0=po[:, :], scalar=cols(C, "conv_pw_b", jo),
                                                          in1=szc[jo][:], op0=ALU.add, op1=ALU.mult),
                  R=["ps5", "colv", f"szc{jo}"], W=[f"mc{jo}"])
            kb.dma("sp", out=mixv[4 + jo, :, tok], in_=mc[jo][:], R=[f"mc{jo}"], W=["mixT_C"])
        yield


def run_interleaved(*gens):
    live = [g_ for g_ in gens if g_ is not None]
    while live:
        for g_ in list(live):
            try:
                next(g_)
            except StopIteration:
                live.remove(g_)


def phase_ABC(C, l, xsrc):
    prog = [0]
    run_interleaved(gen_A(C, l, xsrc, prog), gen_B(C, l, prog), gen_C(C, l, prog))


def phase_E(C, l, xsrc, xdst):
    kb, ar = C.kb, C.arena
    wo = ar.alloc("wo", [128, 8, D], BF16)
    wg = ar.alloc("wg", [128, 8, D], BF16)
    wp = ar.alloc("wp", [128, 2, D], BF16)
    load_w_bf16(C, wo[:], C.w_out[l], "wo")
    load_w_bf16(C, wg[:], C.ple_gate[l], "wg")
    load_w_bf16(C, wp[:], C.ple_proj[l], "wp")
    mt = [ar.alloc(f"mtE{i}", [128, 8, 512], BF16) for i in range(2)]
    osbs = [ar.alloc(f"osb{i}", [128, D], F32) for i in range(2)]
    sq = ar.alloc("sqE", [128, D], F32)
    x1 = ar.alloc("x1", [128, D], F32)
    x1b = ar.alloc("x1b", [128, D], BF16)
    x1T = ar.alloc("x1T", [128, 8, 128], BF16)
    pins = [ar.alloc(f"pin{i}", [128, 256], F32) for i in range(2)]
    pbf = ar.alloc("pbf", [128, 256], BF16)
    pT = ar.alloc("pT", [128, 2, 128], BF16)
    sgs = ar.alloc("sgs", [128, D], F32)
    x2 = [ar.alloc(f"x2{i}", [128, D], F32) for i in range(2)]
    ss = ar.alloc("ssE", [128, 4], F32)
    gpost_t = ar.alloc("gpost", [128, D], F32)
    kb.dma("sp", out=gpost_t[:], in_=C.gpp[l][:, D:2 * D], W=["bcv"])
    gpost = gpost_t[:]
    xins = [ar.alloc(f"xin{i}", [128, D], F32) for i in range(2)]
    mixv = C.mixT.rearrange("k p t -> p k t")

    def load_mt(T):
        ms = T % 2
        kb.dma("sp", out=mt[ms][:], in_=mixv[:, :, T * 512:(T + 1) * 512], R=["mixT_A", "mixT_B", "mixT_C", "mixT_D"], W=[f"mtE{ms}"])

    def part_a(tt):
        T, sub = tt // 4, tt % 4
        ms, s_ = T % 2, tt % 2
        tok = slice(tt * 128, (tt + 1) * 128)
        xin, pin, osb = xins[s_], pins[s_], osbs[s_]
        kb.dma("sp", out=xin[:], in_=xsrc[tok, :], W=[f"xin{s_}"])
        kb.dma("sp", out=pin[:], in_=C.p_in[l, tok, :], W=[f"pin{s_}"])
        for half in range(2):
            po = C.ps[half]
            for k in range(8):
                mm(kb, po[:, :], mt[ms][:, k, sub * 128:(sub + 1) * 128], wo[:, k, half * 512:(half + 1) * 512],
                   k == 0, k == 7, [f"mtE{ms}", "wo"], [f"ps{half}"])
            kb.op("act", lambda e: e.copy(osb[:, half * 512:(half + 1) * 512], po[:, :]), R=[f"ps{half}"], W=[f"osb{s_}"])

    def part_b(tt):
        s_ = tt % 2
        tok = slice(tt * 128, (tt + 1) * 128)
        xin, pin, osb = xins[s_], pins[s_], osbs[s_]
        osn, pinn = f"osb{s_}", f"pin{s_}"
        kb.op("act", lambda e: e.activation(out=sq[:], in_=osb[:], func=AF.Square), R=[osn], W=["sqE"])
        kb.op("dve", lambda e: e.reduce_sum(out=ss[:, 0:1], in_=sq[:], axis=AX.X), R=["sqE"], W=["ssE"])
        rsqrt_op(C, ss[:, 1:2], ss[:, 0:1], 1.0 / D, 1e-6, ["ssE"], ["ssE"])
        kb.op("dve", lambda e: e.scalar_tensor_tensor(out=x1[:], in0=osb[:], scalar=ss[:, 1:2], in1=gpost,
                                                      op0=ALU.mult, op1=ALU.mult), R=[osn, "ssE", "bcv"], W=["x1"])
        kb.op("dve", lambda e: e.tensor_tensor(out=x1[:], in0=x1[:], in1=xin[:], op=ALU.add), R=["x1", f"xin{s_}"], W=["x1"])
        kb.op("act", lambda e: e.copy(x1b[:], x1[:]), R=["x1"], W=["x1b"])
        kb.op("act", lambda e: e.copy(pbf[:], pin[:]), R=[pinn], W=["pbf"])
        pb = C.psb[0]
        for k in range(8):
            kb.op("pe", lambda e: e.transpose(pb[:, k * 128:(k + 1) * 128], x1b[:, k * 128:(k + 1) * 128], C.identb[:]),
                  R=["x1b", "identb"], W=["psb0"])
        kb.op("dve", lambda e: e.tensor_copy(x1T[:], pb[:, :].rearrange("p (k t) -> p k t", k=8)), R=["psb0"], W=["x1T"])
        pb2 = C.psb[1]
        for k in range(2):
            kb.op("pe", lambda e: e.transpose(pb2[:, k * 128:(k + 1) * 128], pbf[:, k * 128:(k + 1) * 128], C.identb[:]),
                  R=["pbf", "identb"], W=["psb1"])
        kb.op("dve", lambda e: e.tensor_copy(pT[:], pb2[:, 0:256].rearrange("p (k t) -> p k t", k=2)), R=["psb1"], W=["pT"])
        for half in range(2):
            pg, pe_ = C.ps[2 + half], C.ps[4 + half]
            hs = slice(half * 512, (half + 1) * 512)
            for k in range(8):
                mm(kb, pg[:, :], x1T[:, k, :], wg[:, k, hs], k == 0, k == 7, ["x1T", "wg"], [f"ps{2 + half}"])
            for k in range(2):
                mm(kb, pe_[:, :], pT[:, k, :], wp[:, k, hs], k == 0, k == 1, ["pT", "wp"], [f"ps{4 + half}"])
            kb.op("act", lambda e: e.activation(out=sgs[:, hs], in_=pg[:, :], func=AF.Sigmoid), R=[f"ps{2 + half}"], W=["sgs"])
            kb.op("dve", lambda e: e.tensor_tensor(out=x2[s_][:, hs], in0=pe_[:, :], in1=sgs[:, hs], op=ALU.mult),
                  R=[f"ps{4 + half}", "sgs"], W=[f"x2{s_}"])
        kb.op("dve", lambda e: e.tensor_tensor(out=x2[s_][:], in0=x2[s_][:], in1=x1[:], op=ALU.add), R=[f"x2{s_}", "x1"], W=[f"x2{s_}"])
        kb.dma("sp", out=xdst[tok, :], in_=x2[s_][:], R=[f"x2{s_}"], W=["xdst"])

    load_mt(0)
    part_a(0)
    for tt in range(32):
        if tt + 1 < 32:
            if (tt + 1) % 4 == 0:
                load_mt((tt + 1) // 4)
            part_a(tt + 1)
        part_b(tt)


def phase_N(C, l):
    kb, ar = C.kb, C.arena
    EXP = AF.Exp
    wq = ar.alloc("wq", [128, 8, 256], BF16)
    wkv = ar.alloc("wkv", [128, 8, 384], BF16)
    wgz = ar.alloc("wgz", [128, 8, 268], BF16)
    load_w_bf16(C, wq[:], C.w_in[l][:, O_Q:O_Q + 256], "wq")
    load_w_bf16(C, wkv[:], C.w_in[l][:, O_KC:O_KC + 384], "wkv")
    load_w_bf16(C, wgz[:], C.w_in[l][:, O_G:O_G + 268], "wgz")
    kselT = ar.alloc("kselT", [64, S], BF16)
    kwinT = ar.alloc("kwinT", [64, S], BF16)
    vaugS = ar.alloc("vaugS", [128, 32, 65], BF16)
    vaugW = ar.alloc("vaugW", [128, 32, 65], BF16)
    kcT = ar.alloc("kcT", [64, 256], BF16)
    vcaug = ar.alloc("vcaug", [128, 2, 129], BF16)
    Xbig = ar.alloc("Xbig", [64, S], BF16)
    Gs = ar.alloc("Gs", [128, 4, 1024], F32)
    Gw = ar.alloc("Gw", [128, 4, 1408], F32)
    kb.dma("pool", out=Xbig[:], in_=C.xbig[:, :], W=["Xbig"])
    kb.dma("sp", out=Gs[:], in_=C.gs[:, :, :], W=["Gs"])
    kb.dma("sp", out=Gw[:], in_=C.gw[:, :, :], W=["Gw"])
    kb.op("dve", lambda e: e.memset(vcaug[:], 0.0), W=["vcaug"])
    kb.op("dve", lambda e: e.memset(vcaug[:, :, 64:65], 1.0), W=["vcaug"])
    kb.dma("pool", out=vcaug[:, :, 65:129], in_=C.mmat.rearrange("(t p) j -> p t j", p=128), R=["vcaug"], W=["vcaug"])
    kb.op("dve", lambda e: e.memset(vaugS[:, :, 64:65], 1.0), W=["vaugS"])
    kb.op("dve", lambda e: e.memset(vaugW[:, :, 64:65], 1.0), W=["vaugW"])
    mark = ar.mark()
    kcmpT = ar.alloc("kcmpT", [64, S], BF16)
    vcmpT = ar.alloc("vcmpT", [64, S], BF16)
    cnt = 0
    for (dst, c0, rn) in ((kcmpT, 0, "kcmpT"), (vcmpT, 64, "vcmpT"), (kselT, 128, "kselT"), (kwinT, 256, "kwinT")):
        for T in range(8):
            b = cnt % 2; cnt += 1
            pp = C.ps[b]
            for k in range(8):
                mm(kb, pp[0:64, :], wkv[:, k, c0:c0 + 64], C.hT[:, k, T * 512:(T + 1) * 512], k == 0, k == 7, ["hT", "wkv"], [f"ps{b}"])
            eng = "act" if b == 0 else "dve"
            if eng == "act":
                kb.op("act", lambda e: e.copy(dst[:, T * 512:(T + 1) * 512], pp[0:64, :]), R=[f"ps{b}"], W=[rn])
            else:
                kb.op("dve", lambda e: e.tensor_copy(dst[:, T * 512:(T + 1) * 512], pp[0:64, :]), R=[f"ps{b}"], W=[rn])
    for tt in range(32):
        b = tt % 2
        pp = C.ps[2 + b]
        tok = slice(tt * 128, (tt + 1) * 128)
        for (c0, o0) in ((192, 0), (320, 64)):
            for k in range(8):
                mm(kb, pp[:, o0:o0 + 64], C.hT[:, k, tok], wkv[:, k, c0:c0 + 64], k == 0, k == 7, ["hT", "wkv"], [f"ps{2 + b}"])
        kb.op("act", lambda e: e.copy(vaugS[:, tt, 0:64], pp[:, 0:64]), R=[f"ps{2 + b}"], W=["vaugS"])
        kb.op("dve", lambda e: e.tensor_copy(vaugW[:, tt, 0:64], pp[:, 64:128]), R=[f"ps{2 + b}"], W=["vaugW"])
    w1b = ar.alloc("w1b", [64, 2, 32, 128], BF16)
    w2b = ar.alloc("w2b", [128, 2, 64], BF16)
    posf = ar.alloc("posf", [64, 2, 32], F32)
    posb = ar.alloc("posb", [64, 2, 32, 2], BF16)
    hid = ar.alloc("hid", [128, 256], BF16)
    bsb = ar.alloc("bsb", [128, 2], F32)
    for kv in range(2):
        kb.dma("pool", out=w1b[:, kv], in_=C.nsa_w1[l, kv].rearrange("(j d) h -> d j h", d=64), W=["w1b"])
        kb.dma("pool", out=w2b[:, kv, :], in_=C.nsa_w2[l, kv], W=["w2b"])
    kb.dma("sp", out=posf[:], in_=C.nsa_posT[l].rearrange("v d j -> d v j"), W=["posf"])
    for dup in range(2):
        kb.op("dve", lambda e: e.tensor_copy(posb[:, :, :, dup], posf[:]), R=["posf"], W=["posb"])
    for kv in range(2):
        src = kcmpT if kv == 0 else vcmpT
        srn = "kcmpT" if kv == 0 else "vcmpT"
        sv = src[:, :].rearrange("p (b s) -> p b s", s=16)
        ph, pbias, po = C.ps[0], C.ps[1], C.ps[4]
        for j in range(32):
            mm(kb, ph[:, 0:255], w1b[:, kv, j, :], sv[:, (j // 16):(j // 16) + 255, j % 16], j == 0, j == 31, ["w1b", srn], ["ps0"])
        for j in range(32):
            mm(kb, pbias[:, 0:2], w1b[:, kv, j, :], posb[:, kv, j, :], j == 0, j == 31, ["w1b", "posb"], ["ps1"])
        kb.op("dve", lambda e: e.tensor_copy(bsb[:, 0:2], pbias[:, 0:2]), R=["ps1"], W=["bsb"])
        kb.op("act", lambda e: e.activation(out=hid[:, 0:255], in_=ph[:, 0:255], func=AF.Silu, bias=bsb[:, 0:1]), R=["ps0", "bsb"], W=["hid"])
        if kv == 0:
            mm(kb, po[0:64, 0:255], w2b[:, 0, :], hid[:, 0:255], True, True, ["w2b", "hid"], ["ps4"])
            kb.op("dve", lambda e: e.tensor_copy(kcT[:, 0:255], po[0:64, 0:255]), R=["ps4"], W=["kcT"])
        else:
            for nt in range(2):
                nn = 128 if nt == 0 else 127
                mm(kb, po[0:nn, nt * 64:(nt + 1) * 64], hid[:, nt * 128:nt * 128 + nn], w2b[:, 1, :], True, True, ["w2b", "hid"], ["ps4"])
                kb.op("dve", lambda e: e.tensor_copy(vcaug[0:nn, nt, 0:64], po[0:nn, nt * 64:(nt + 1) * 64]), R=["ps4"], W=["vcaug"])
    kb.barrier()
    ar.reset(mark)
    qT = ar.alloc("qT", [64, 4, 512], BF16)
    gsb = ar.alloc("gsb", [128, 4, 12], F32)
    sza = ar.alloc("sza", [128, 4, 256], F32)
    acc = ar.alloc("acco", [128, 4, 256], F32)
    imp = ar.alloc("imp", [128, 4, 64], F32)
    imp2 = ar.alloc("imp2", [128, 64], F32)
    tmpk = ar.alloc("tmpk", [128, 64], F32)
    selm = ar.alloc("selm", [128, 64], F32)
    selb = ar.alloc("selb", [128, 64], BF16)
    m8 = ar.alloc("m8", [128, 16], F32)
    tkm = ar.alloc("tkm", [128, 2, 4, 64], F32)
    negT = ar.alloc("negT", [64, 512], BF16)
    cb = [ar.alloc(f"cb{i}", [128, 512], F32) for i in range(2)]
    s2 = [ar.alloc(f"s2{i}", [128, 512], F32) for i in range(2)]
    eT = [ar.alloc(f"eT{i}", [128, 512], BF16) for i in range(4)]
    eC = [ar.alloc(f"eC{i}", [128, 512], BF16) for i in range(2)]
    rd = ar.alloc("rd", [128, 8], F32)
    rd4 = ar.alloc("rd4", [128, 4, 2], F32)
    osb4 = [ar.alloc(f"osb4{i}", [128, 4, 65], F32) for i in range(2)]
    fcnt = [0]
    ma = ar.alloc("ma", [128, 256], BF16)
    mt = [ar.alloc(f"mtN{i}", [128, 2, 128], BF16) for i in range(2)]
    identf = C.cstf[:, 0:128]
    mixv = C.mixT.rearrange("k p t -> p k t")
    sbanks = [(C.ps[0], "ps0"), (C.ps[1], "ps1"), (C.psb[1][:, :].bitcast(F32), "psb1")]
    sc = [0]
    ec = [0]
    oc = [0]
    cbc = [0]
    for Q in range(8):
        qtok = slice(Q * 512, (Q + 1) * 512)
        for h in range(4):
            b = sc[0] % 3; sc[0] += 1
            pp, ppn = sbanks[b]
            for k in range(8):
                mm(kb, pp[0:64, :], wq[:, k, h * 64:(h + 1) * 64], C.hT[:, k, qtok], k == 0, k == 7, ["hT", "wq"], [ppn])
            kb.op("act", lambda e: e.copy(qT[:, h, :], pp[0:64, :]), R=[ppn], W=["qT"])
        for sub in range(4):
            b = sc[0] % 3; sc[0] += 1
            pp, ppn = sbanks[b]
            tok = slice(Q * 512 + sub * 128, Q * 512 + (sub + 1) * 128)
            for k in range(8):
                mm(kb, pp[:, 0:268], C.hT[:, k, tok], wgz[:, k, :], k == 0, k == 7, ["hT", "wgz"], [ppn])
            kb.op("act", lambda e: e.activation(out=gsb[:, sub, :], in_=pp[:, 0:12], func=AF.Sigmoid), R=[ppn], W=["gsb"])
            kb.op("act", lambda e: e.activation(out=sza[:, sub, :], in_=pp[:, 12:268], func=AF.Silu), R=[ppn], W=["sza"])
        kb.dma("sp", out=tkm[:, 0], in_=C.tk_mul[4 * Q:4 * Q + 4].rearrange("c p j -> p c j"), W=["tkm"])
        kb.dma("sp", out=tkm[:, 1], in_=C.tk_add[4 * Q:4 * Q + 4].rearrange("c p j -> p c j"), W=["tkm"])

        def finish_branch(h, gi):
            fs = fcnt[0] % 2; fcnt[0] += 1
            ob4 = osb4[fs]
            o0 = (h % 2) * 128
            for sub in range(4):
                po = C.ps[2 + sub]
                if sub % 2 == 0:
                    kb.op("act", lambda e: e.copy(ob4[:, sub, :], po[:, o0:o0 + 65]), R=[f"ps{2 + sub}"], W=[f"osb4{fs}"])
                else:
                    kb.op("dve", lambda e: e.tensor_copy(ob4[:, sub, :], po[:, o0:o0 + 65]), R=[f"ps{2 + sub}"], W=[f"osb4{fs}"])
            kb.op("dve", lambda e: e.tensor_scalar(out=rd4[:, :, 0:1], in0=ob4[:, :, 64:65], scalar1=1e-30, scalar2=None, op0=ALU.add),
                  R=[f"osb4{fs}"], W=["rd4"])
            kb.op("dve", lambda e: e.reciprocal(out=rd4[:, :, 0:1], in_=rd4[:, :, 0:1]), R=["rd4"], W=["rd4"])
            kb.op("dve", lambda e: e.tensor_tensor(out=rd4[:, :, 1:2], in0=rd4[:, :, 0:1], in1=gsb[:, :, h * 3 + gi:h * 3 + gi + 1], op=ALU.mult),
                  R=["rd4", "gsb"], W=["rd4"])
            hs = slice(h * 64, (h + 1) * 64)
            kb.op("dve", lambda e: e.tensor_tensor(out=ob4[:, :, 0:64], in0=ob4[:, :, 0:64], in1=rd4[:, :, 1:2].to_broadcast([128, 4, 64]), op=ALU.mult),
                  R=[f"osb4{fs}", "rd4"], W=[f"osb4{fs}"])
            kb.op("dve", lambda e: e.tensor_tensor(out=acc[:, :, hs], in0=acc[:, :, hs], in1=ob4[:, :, 0:64], op=ALU.add),
                  R=[f"osb4{fs}", "acco"], W=["acco"])

        nvis = min(255, 32 * Q + 31)
        tiles = [(0, min(128, nvis))] + ([(1, nvis - 128)] if nvis > 128 else [])
        for h in range(4):
            for (nt, nn) in tiles:
                cs = cbc[0] % 2; cbc[0] += 1
                kb.dma("sp", out=cb[cs][0:nn, :], in_=C.cbias[h, nt * 128:nt * 128 + nn, qtok], W=[f"cb{cs}"])
                b = sc[0] % 3; sc[0] += 1
                pp, ppn = sbanks[b]
                mm(kb, pp[0:nn, :], kcT[:, nt * 128:nt * 128 + nn], qT[:, h, :], True, True, ["kcT", "qT"], [ppn])
                kb.op("dve", lambda e: e.scalar_tensor_tensor(out=s2[cs][0:nn, :], in0=pp[0:nn, :], scalar=0.125, in1=cb[cs][0:nn, :],
                                                              op0=ALU.mult, op1=ALU.add), R=[ppn, f"cb{cs}"], W=[f"s2{cs}"])
                kb.op("act", lambda e: e.activation(out=eC[nt][0:nn, :], in_=s2[cs][0:nn, :], func=EXP), R=[f"s2{cs}"], W=[f"eC{nt}"])
            for half in range(2):
                ob = 2 + oc[0] % 2; oc[0] += 1
                po = C.ps[ob]
                for s_i in range(2):
                    sub = half * 2 + s_i
                    for ti, (nt, nn) in enumerate(tiles):
                        mm(kb, po[:, s_i * 129:(s_i + 1) * 129], eC[nt][0:nn, sub * 128:(sub + 1) * 128], vcaug[0:nn, nt, :],
                           ti == 0, ti == len(tiles) - 1, [f"eC{nt}", "vcaug"], [f"ps{ob}"])
                for s_i in range(2):
                    sub = half * 2 + s_i
                    o0 = s_i * 129
                    kb.op("dve", lambda e: e.tensor_scalar(out=rd[:, 0:1], in0=po[:, o0 + 64:o0 + 65], scalar1=1e-30, scalar2=None, op0=ALU.add),
                          R=[f"ps{ob}"], W=["rd"])
                    kb.op("dve", lambda e: e.reciprocal(out=rd[:, 0:1], in_=rd[:, 0:1]), R=["rd"], W=["rd"])
                    kb.op("dve", lambda e: e.tensor_tensor(out=rd[:, 1:2], in0=rd[:, 0:1], in1=gsb[:, sub, h * 3:h * 3 + 1], op=ALU.mult),
                          R=["rd", "gsb"], W=["rd"])
                    hs = slice(h * 64, (h + 1) * 64)
                    kb.op("dve", lambda e: e.tensor_scalar(out=acc[:, sub, hs], in0=po[:, o0:o0 + 64], scalar1=rd[:, 1:2], scalar2=None, op0=ALU.mult),
                          R=[f"ps{ob}", "rd"], W=["acco"])
                    if h == 0:
                        kb.op("dve", lambda e: e.tensor_scalar(out=imp[:, sub, :], in0=po[:, o0 + 65:o0 + 129], scalar1=rd[:, 0:1], scalar2=None, op0=ALU.mult),
                              R=[f"ps{ob}", "rd"], W=["imp"])
                    else:
                        kb.op("dve", lambda e: e.scalar_tensor_tensor(out=imp[:, sub, :], in0=po[:, o0 + 65:o0 + 129], scalar=rd[:, 0:1], in1=imp[:, sub, :],
                                                                      op0=ALU.mult, op1=ALU.add), R=[f"ps{ob}", "rd", "imp"], W=["imp"])
        if C.debug:
            kb.dma("sp", out=C.dbg_y3[0, qtok, :].rearrange("(s p) c -> p s c", p=128), in_=acc[:], R=["acco"], W=["dbg_y3"])
        for sub in range(4):
            kb.op("dve", lambda e: e.tensor_tensor(out=imp2[:], in0=imp[:, sub, :], in1=tkm[:, 0, sub, :], op=ALU.mult), R=["imp", "tkm"], W=["imp2"])
            kb.op("dve", lambda e: e.tensor_tensor(out=imp2[:], in0=imp2[:], in1=tkm[:, 1, sub, :], op=ALU.add), R=["imp2", "tkm"], W=["imp2"])
            kb.op("dve", lambda e: e.max(out=m8[:, 0:8], in_=imp2[:]), R=["imp2"], W=["m8"])
            kb.op("dve", lambda e: e.match_replace(out=tmpk[:], in_to_replace=m8[:, 0:8], in_values=imp2[:], imm_value=-1e30), R=["m8", "imp2"], W=["tmpk"])
            kb.op("dve", lambda e: e.max(out=m8[:, 8:16], in_=tmpk[:]), R=["tmpk"], W=["m8"])
            kb.op("dve", lambda e: e.tensor_scalar(out=m8[:, 15:16], in0=m8[:, 15:16], scalar1=-1e29, scalar2=None, op0=ALU.max), R=["m8"], W=["m8"])
            kb.op("dve", lambda e: e.tensor_scalar(out=selm[:], in0=imp2[:], scalar1=m8[:, 15:16], scalar2=None, op0=ALU.is_ge), R=["imp2", "m8"], W=["selm"])
            kb.op("dve", lambda e: e.tensor_copy(selb[:], selm[:]), R=["selm"], W=["selb"])
            pbs = 0
            pt = C.psb[pbs]
            kb.op("pe", lambda e: e.transpose(pt[0:64, 0:128], selb[:], C.identb[:]), R=["selb", "identb"], W=[f"psb{pbs}"])
            kb.op("dve", lambda e: e.tensor_scalar(out=negT[:, sub * 128:(sub + 1) * 128], in0=pt[0:64, 0:128], scalar1=1.0, scalar2=-NEGM,
                                                   op0=ALU.subtract, op1=ALU.mult), R=[f"psb{pbs}"], W=["negT"])
            if C.debug:
                kb.dma("sp", out=C.dbg_sel[Q * 512 + sub * 128:Q * 512 + (sub + 1) * 128, :], in_=selm[:], R=["selm"], W=["dbg_sel"])
        for branch in ("sel", "win"):
            kT, krn = (kselT, "kselT") if branch == "sel" else (kwinT, "kwinT")
            vA, vrn = (vaugS, "vaugS") if branch == "sel" else (vaugW, "vaugW")
            G, grn = (Gs, "Gs") if branch == "sel" else (Gw, "Gw")
            gi = 1 if branch == "sel" else 2
            kt_lo = 0 if branch == "sel" else max(0, 4 * Q - 4)
            for h in range(4):
                pendq = []
                for kt in range(kt_lo, 4 * Q + 4):
                    b = sc[0] % 3; sc[0] += 1
                    pp, ppn = sbanks[b]
                    ksl = slice(kt * 128, (kt + 1) * 128)
                    if branch == "sel":
                        mm(kb, pp[:, :], kT[:, ksl], qT[:, h, :], True, False, [krn, "qT"], [ppn])
                        mm(kb, pp[:, :], Xbig[:, ksl], negT[:, :], False, True, ["Xbig", "negT"], [ppn])
                    else:
                        mm(kb, pp[:, :], kT[:, ksl], qT[:, h, :], True, True, [krn, "qT"], [ppn])
                    if len(pendq) >= 2:
                        for f_ in pendq.pop(0):
                            f_()
                    pend = []
                    pendq.append(pend)
                    ei = ec[0] % 4; ec[0] += 1
                    ktrel = kt - 4 * Q
                    if branch == "sel" and ktrel <= -2:
                        kb.op("act", lambda e: e.activation(out=eT[ei][:], in_=pp[:, :], func=EXP, bias=cols(C, "relb31", h), scale=0.125),
                              R=[ppn, "colv"], W=[f"eT{ei}"])
                    else:
                        si = ec[0] % 2
                        u0 = 384 - 128 * ktrel
                        kb.op("dve", lambda e: e.scalar_tensor_tensor(out=s2[si][:], in0=pp[:, :], scalar=0.125, in1=G[:, h, u0:u0 + 512],
                                                                      op0=ALU.mult, op1=ALU.add), R=[ppn, grn], W=[f"s2{si}"])
                        kb.op("act", lambda e: e.activation(out=eT[ei][:], in_=s2[si][:], func=EXP), R=[f"s2{si}"], W=[f"eT{ei}"])
                    for sub in range(4):
                        c = 4 * Q + sub
                        lo = 0 if branch == "sel" else max(0, c - 4)
                        if kt < lo or kt > c:
                            continue
                        o0 = (h % 2) * 128

                        def pv(sub=sub, o0=o0, ei=ei, kt=kt, lo=lo, c=c):
                            mm(kb, C.ps[2 + sub][:, o0:o0 + 65], eT[ei][:, sub * 128:(sub + 1) * 128], vA[:, kt, :], kt == lo, kt == c,
                               [f"eT{ei}", vrn], [f"ps{2 + sub}"])
                        pend.append(pv)
                for pl_ in pendq:
                    for f_ in pl_:
                        f_()
                finish_branch(h, gi)
            if C.debug and branch == "sel":
                kb.dma("sp", out=C.dbg_y3[1, qtok, :].rearrange("(s p) c -> p s c", p=128), in_=acc[:], R=["acco"], W=["dbg_y3"])
        for sub in range(4):
            tt = 4 * Q + sub
            tok = slice(tt * 128, (tt + 1) * 128)
            kb.op("dve", lambda e: e.tensor_tensor(out=ma[:], in0=acc[:, sub, :], in1=sza[:, sub, :], op=ALU.mult), R=["acco", "sza"], W=["ma"])
            s_ = tt % 2
            pb = C.psb[0]
            for j in range(2):
                kb.op("pe", lambda e: e.transpose(pb[:, j * 128:(j + 1) * 128], ma[:, j * 128:(j + 1) * 128], C.identb[:]),
                      R=["ma", "identb"], W=["psb0"])
            kb.op("act", lambda e: e.copy(mt[s_][:], pb[:, 0:256].rearrange("p (k t) -> p k t", k=2)), R=["psb0"], W=[f"mtN{s_}"])
            kb.dma("sp", out=mixv[:, 0:2, tok], in_=mt[s_][:], R=[f"mtN{s_}"], W=["mixT_A"])
            if C.debug:
                kb.dma("sp", out=C.dbg_ya[tok, :], in_=acc[:, sub, :], R=["acco"], W=["dbg_ya"])


def phase_R(C, l):
    kb, ar = C.kb, C.arena
    C0 = math.exp(-0.5)
    dve = lambda fn, R, W: kb.op("dve", fn, R=R, W=W)
    act = lambda fn, R, W: kb.op("act", fn, R=R, W=W)
    bank = [0]

    def nb():
        b = bank[0] % 6
        bank[0] += 1
        return b, C.ps[b], f"ps{b}"

    wR = ar.alloc("wR", [128, 8, 832], BF16)
    wZ = ar.alloc("wZ", [128, 8, 256], BF16)
    load_w_bf16(C, wR[:], C.w_in[l][:, O_RW:O_RW + 832], "wR")
    load_w_bf16(C, wZ[:], C.w_in[l][:, O_ZD:O_ZD + 256], "wZ")
    wup = ar.alloc("wup", [32, 2, 256], F32)
    kb.dma("sp", out=wup[:, 0, :], in_=C.rwkv_w_up[l], W=["wup"])
    kb.dma("sp", out=wup[:, 1, :], in_=C.rwkv_a_up[l], W=["wup"])
    rmk = ar.alloc("rmk", [64, 4, 512], F32)
    kb.dma("sp", out=rmk[:], in_=C.rmasks[:, :, :], W=["rmk"])
    MUS, MUI, MLS, IDR = rmk[:, 0, :], rmk[:, 1, :], rmk[:, 2, :], rmk[:, 3, :]
    rst = ar.alloc("rst", [64, 512], F32)
    dve(lambda e: e.memset(rst[:], 1.0), [], ["rst"])
    dve(lambda e: e.memset(rst[:].rearrange("p (a b) -> p a b", b=64)[:, :, 0:1], 0.0), ["rst"], ["rst"])
    ones64 = ar.alloc("ones64", [64, 64], F32)
    dve(lambda e: e.memset(ones64[:], 1.0), [], ["ones64"])
    ident64 = C.cstf[0:64, 0:64]
    carry = ar.alloc("carry", [64, 16], F32)
    dve(lambda e: e.memset(carry[:], 0.0), [], ["carry"])
    ST = [ar.alloc(f"ST{i}", [64, 4, 64], F32) for i in range(2)]
    dve(lambda e: e.memset(ST[0][:], 0.0), [], ["ST0"])
    omka = ar.alloc("omka", [64, 4], F32)
    cv = lambda name: C.colv[0:64, C.col_off[name][0]:C.col_off[name][0] + C.col_off[name][1]]
    dve(lambda e: e.tensor_scalar(out=omka[:], in0=cv("rw_ka"), scalar1=-1.0, scalar2=1.0, op0=ALU.mult, op1=ALU.add), ["colv"], ["omka"])
    f4 = lambda name, dt=F32: ar.alloc(name, [64, 4, 128], dt)
    RW = ar.alloc("RW", [64, 12, 512], F32)
    WD = ar.alloc("WD", [32, 512], F32)
    AD = ar.alloc("AD", [32, 512], F32)
    raw = [ar.alloc(f"raw{i}", [64, 516], F32) for i in range(2)]
    tmpg = [ar.alloc(f"tmpg{i}", [64, 512], F32) for i in range(2)]
    sig, aa, cum, Pinv, Pex = f4("sig"), f4("aa"), f4("cum"), f4("Pinv"), f4("Pex")
    kk, kp, t4 = f4("kk"), f4("kp"), f4("t4")
    AT, BT, KT, BPf, KPf, vb = f4("AT", BF16), f4("BT", BF16), f4("KT", BF16), f4("BPf", BF16), f4("KPf", BF16), f4("vb", BF16)
    p8 = lambda name: ar.alloc(name, [64, 8, 64], BF16)
    A_tm = p8("A_tm")
    X, XT, Xn, XTn, TT = p8("X"), p8("XT"), p8("Xn"), p8("XTn"), p8("TT")
    AKT, Wb = p8("AKT"), p8("Wb")
    D2 = {}
    D2["Pt"] = [f4("Pt" + str(i)) for i in range(2)]
    D2["RT"] = [f4("RT" + str(i), BF16) for i in range(2)]
    for nm_ in ("BP_tm", "KP_tm", "V_tm", "RBT", "RKT", "GT", "Hb"):
        D2[nm_] = [p8(nm_ + str(i)) for i in range(2)]
    D2["rkb"] = [ar.alloc(f"rkb{i}", [64, 8, 2], F32) for i in range(2)]
    Usb = ar.alloc("Usb", [64, 4, 64], BF16)
    STb = [ar.alloc(f"STb{i}", [64, 4, 64], BF16) for i in range(2)]
    dve(lambda e: e.memset(STb[0][:], 0.0), [], ["STb0"])
    identb64 = C.identb[0:64, 0:64]
    Y = ar.alloc("Y", [64, 4, 64], F32)
    Yc = ar.alloc("Yc", [64, 4, 64], F32)
    Ysq = ar.alloc("Ysq", [64, 4, 64], F32)
    stt = ar.alloc("stt", [64, 16], F32)
    szd = ar.alloc("szd", [64, 256], F32)
    md = ar.alloc("md", [64, 256], BF16)
    mtR = [ar.alloc(f"mtR{i}", [128, 2, 64], BF16) for i in range(2)]
    gng, gnb = bcs(C, "gn_g")[0:64, :], bcs(C, "gn_b")[0:64, :]
    mixv = C.mixT.rearrange("k p t -> p k t")
    bcol = lambda name: cv(name).rearrange("p (h o) -> p h o", o=1).to_broadcast([64, 4, 128])
    flat = lambda t: t[:].rearrange("p h t -> p (h t)")
    v8 = lambda t: t[:].rearrange("p h (c t) -> p (h c) t", t=64)
    groups = [(g * 64, 64) for g in range(12)] + [(768, 32), (800, 32)]
    def stage1(tt):
        par = tt % 2
        PAR = str(par)
        tok = slice(tt * 128, (tt + 1) * 128)
        Pt, RT = D2["Pt"][par], D2["RT"][par]
        BP_tm, KP_tm, V_tm, RBT, RKT, GT, Hb, rkb = (D2[k_][par] for k_ in ("BP_tm", "KP_tm", "V_tm", "RBT", "RKT", "GT", "Hb", "rkb"))
        Pend, kka = cum, sig
        sub4 = tt % 4
        if sub4 == 0:
            tok512 = slice(tt * 128, tt * 128 + 512)
            for g, (c0, M) in enumerate(groups):
                b, pp, prn = nb()
                for k in range(8):
                    mm(kb, pp[0:M, 0:512], wR[:, k, c0:c0 + M], C.hT[:, k, tok512], k == 0, k == 7, ["hT", "wR"], [prn])
                rs = g % 2
                rw_, rrn = raw[rs], f"raw{rs}"
                act(lambda e: e.copy(rw_[0:M, 1:513], pp[0:M, 0:512]), [prn], [rrn])
                act(lambda e: e.copy(rw_[0:M, 0:1], carry[0:M, g:g + 1]), ["carry"], [rrn])
                dve(lambda e: e.tensor_tensor(out=tmpg[rs][0:M, :], in0=rw_[0:M, 0:512], in1=rw_[0:M, 1:513], op=ALU.subtract), [rrn], [f"tmpg{rs}"])
                dst = RW[:, g, :] if g < 12 else (WD[:, :] if g == 12 else AD[:, :])
                drn = "RW" if g < 12 else ("WD" if g == 12 else "AD")
                dve(lambda e: e.scalar_tensor_tensor(out=dst, in0=tmpg[rs][0:M, :], scalar=cv("rw_mu")[0:M, g:g + 1], in1=rw_[0:M, 1:513],
                                                     op0=ALU.mult, op1=ALU.add), [f"tmpg{rs}", rrn, "colv"], [drn])
                act(lambda e: e.copy(carry[0:M, g:g + 1], rw_[0:M, 512:513]), [rrn], ["carry"])
                if g % 2 == 1:
                    yield
            act(lambda e: e.activation(out=WD[:, :], in_=WD[:, :], func=AF.Tanh), ["WD"], ["WD"])
        ts4 = slice(sub4 * 128, (sub4 + 1) * 128)
        r_, k_, v_ = RW[:, 0:4, ts4], RW[:, 4:8, ts4], RW[:, 8:12, ts4]
        WDv, ADv = WD[:, ts4], AD[:, ts4]
        yield
        b1, pz, pzn = nb()
        b2, pa, pan = nb()
        for h in range(4):
            mm(kb, pz[0:64, h * 128:(h + 1) * 128], wup[:, 0, h * 64:(h + 1) * 64], WDv, True, True, ["wup", "WD"], [pzn])
        for h in range(4):
            mm(kb, pa[0:64, h * 128:(h + 1) * 128], wup[:, 1, h * 64:(h + 1) * 64], ADv, True, True, ["wup", "AD"], [pan])
        for h in range(4):
            act(lambda e: e.activation(out=sig[:, h, :], in_=pz[0:64, h * 128:(h + 1) * 128], func=AF.Sigmoid, bias=cv("rw_w0")[:, h:h + 1]),
                [pzn, "colv"], ["sig"])
            act(lambda e: e.activation(out=aa[:, h, :], in_=pa[0:64, h * 128:(h + 1) * 128], func=AF.Sigmoid, bias=cv("rw_a0")[:, h:h + 1]),
                [pan, "colv"], ["aa"])
        yield
        dve(lambda e: e.tensor_tensor_scan(out=flat(cum), data0=rst[:], data1=flat(sig), initial=0.0, op0=ALU.mult, op1=ALU.add),
            ["rst", "sig"], ["cum"])
        act(lambda e: e.activation(out=flat(Pt), in_=flat(cum), func=AF.Exp, scale=-C0), ["cum"], ["Pt" + PAR])
        act(lambda e: e.activation(out=flat(Pinv), in_=flat(cum), func=AF.Exp, scale=C0), ["cum"], ["Pinv"])
        dve(lambda e: e.tensor_tensor(out=flat(t4), in0=flat(cum), in1=flat(sig), op=ALU.subtract), ["cum", "sig"], ["t4"])
        act(lambda e: e.activation(out=flat(Pex), in_=flat(t4), func=AF.Exp, scale=-C0), ["t4"], ["Pex"])
        dve(lambda e: e.tensor_tensor(out=v8(Pend), in0=v8(Pinv), in1=v8(Pt)[:, :, 63:64].to_broadcast([64, 8, 64]), op=ALU.mult),
            ["Pinv", "Pt" + PAR], ["cum"])
        yield
        dve(lambda e: e.tensor_tensor(out=kk[:], in0=k_, in1=bcol("rw_kk"), op=ALU.mult), ["RW", "colv"], ["kk"])
        act(lambda e: e.activation(out=flat(t4), in_=flat(kk), func=AF.Square), ["kk"], ["t4"])
        b, pn, pnn = nb()
        mm(kb, pn[0:64, :], ones64[:], flat(t4), True, True, ["ones64", "t4"], [pnn])
        act(lambda e: e.activation(out=flat(t4), in_=pn[0:64, :], func=AF.Sqrt), [pnn], ["t4"])
        dve(lambda e: e.tensor_scalar(out=flat(t4), in0=flat(t4), scalar1=1e-12, scalar2=None, op0=ALU.max), ["t4"], ["t4"])
        dve(lambda e: e.reciprocal(out=flat(t4), in_=flat(t4)), ["t4"], ["t4"])
        dve(lambda e: e.tensor_tensor(out=flat(kk), in0=flat(kk), in1=flat(t4), op=ALU.mult), ["kk", "t4"], ["kk"])
        dve(lambda e: e.tensor_tensor(out=kp[:], in0=aa[:], in1=bcol("rw_ka"), op=ALU.mult), ["aa", "colv"], ["kp"])
        dve(lambda e: e.tensor_tensor(out=kp[:], in0=kp[:], in1=omka[:].rearrange("p (h o) -> p h o", o=1).to_broadcast([64, 4, 128]), op=ALU.add),
            ["kp", "omka"], ["kp"])
        dve(lambda e: e.tensor_tensor(out=kp[:], in0=kp[:], in1=k_, op=ALU.mult), ["kp", "RW"], ["kp"])
        dve(lambda e: e.scalar_tensor_tensor(out=flat(AT), in0=flat(kk), scalar=-1.0, in1=flat(Pex), op0=ALU.mult, op1=ALU.mult), ["kk", "Pex"], ["AT"])
        dve(lambda e: e.tensor_tensor(out=flat(kka), in0=flat(kk), in1=flat(aa), op=ALU.mult), ["kk", "aa"], ["sig"])
        dve(lambda e: e.tensor_tensor(out=flat(BT), in0=flat(kka), in1=flat(Pinv), op=ALU.mult), ["sig", "Pinv"], ["BT"])
        dve(lambda e: e.tensor_tensor(out=flat(KT), in0=flat(kp), in1=flat(Pinv), op=ALU.mult), ["kp", "Pinv"], ["KT"])
        dve(lambda e: e.tensor_tensor(out=RT[:], in0=r_, in1=Pt[:], op=ALU.mult), ["RW", "Pt" + PAR], ["RT" + PAR])
        dve(lambda e: e.tensor_tensor(out=flat(BPf), in0=flat(kka), in1=flat(Pend), op=ALU.mult), ["sig", "cum"], ["BPf"])
        dve(lambda e: e.tensor_tensor(out=flat(KPf), in0=flat(kp), in1=flat(Pend), op=ALU.mult), ["kp", "cum"], ["KPf"])
        dve(lambda e: e.tensor_tensor(out=t4[:], in0=r_, in1=kp[:], op=ALU.mult), ["RW", "kp"], ["t4"])
        dve(lambda e: e.tensor_tensor(out=t4[:], in0=t4[:], in1=bcol("rw_rk"), op=ALU.mult), ["t4", "colv"], ["t4"])
        b, pr, prn_ = nb()
        for c2 in range(2):
            for h in range(4):
                p = c2 * 4 + h
                mm(kb, pr[0:64, p * 2:p * 2 + 2], t4[:, h, c2 * 64:(c2 + 1) * 64], ones64[:, 0:2], True, True, ["t4", "ones64"], [prn_])
        dve(lambda e: e.tensor_copy(rkb[:].rearrange("p a b -> p (a b)"), pr[0:64, 0:16]), [prn_], ["rkb" + PAR])
        yield
        act(lambda e: e.copy(vb[:], v_), ["RW"], ["vb"])
        for qi, (srcf, srn, dstt, drn) in enumerate(((lambda h, c2: AT[:, h, c2 * 64:(c2 + 1) * 64], "AT", A_tm, "A_tm"),
                                                    (lambda h, c2: BPf[:, h, c2 * 64:(c2 + 1) * 64], "BPf", BP_tm, "BP_tm" + PAR),
                                                    (lambda h, c2: KPf[:, h, c2 * 64:(c2 + 1) * 64], "KPf", KP_tm, "KP_tm" + PAR),
                                                    (lambda h, c2: vb[:, h, c2 * 64:(c2 + 1) * 64], "vb", V_tm, "V_tm" + PAR))):
            pt, ptn = C.psb[qi % 2], f"psb{qi % 2}"
            for c2 in range(2):
                for h in range(4):
                    p = c2 * 4 + h
                    kb.op("pe", lambda e: e.transpose(pt[0:64, p * 64:(p + 1) * 64], srcf(h, c2), identb64), R=[srn, "identb"], W=[ptn])
            act(lambda e: e.copy(dstt[:].rearrange("p a b -> p (a b)"), pt[0:64, 0:512]), [ptn], [drn])
            yield
        yield
        fm = lambda t, h, c2: t[:, h, c2 * 64:(c2 + 1) * 64]
        for (lt, ltn, rt_, rtn, msk, dstt, drn) in ((AT, "AT", BT, "BT", MLS, X, "X"), (BT, "BT", AT, "AT", MUS, XT, "XT"),
                                                    (KT, "KT", AT, "AT", MUS, AKT, "AKT"), (BT, "BT", RT, "RT" + PAR, MUI, RBT, "RBT" + PAR),
                                                    (KT, "KT", RT, "RT" + PAR, MUI, RKT, "RKT" + PAR)):
            b, pm, pmn = nb()
            for c2 in range(2):
                for h in range(4):
                    p = c2 * 4 + h
                    mm(kb, pm[0:64, p * 64:(p + 1) * 64], fm(lt, h, c2), fm(rt_, h, c2), True, True, [ltn, rtn], [pmn])
            dve(lambda e: e.tensor_tensor(out=dstt[:].rearrange("p a b -> p (a b)"), in0=pm[0:64, :], in1=msk, op=ALU.mult), [pmn, "rmk"], [drn])
            yield
        f8 = lambda t: t[:].rearrange("p a b -> p (a b)")
        dve(lambda e: e.tensor_tensor(out=f8(TT), in0=f8(XT), in1=IDR, op=ALU.add), ["XT", "rmk"], ["TT"])
        yield
        cx, cxn, cxt, cxtn = X, "X", XT, "XT"
        nx, nxn, nxt_, nxtn = Xn, "Xn", XTn, "XTn"
        for lev in range(1, 6):
            b, p2, p2n = nb()
            for p in range(8):
                mm(kb, p2[0:64, p * 64:(p + 1) * 64], cxt[:, p, :], cx[:, p, :], True, True, [cxn, cxtn], [p2n])
            act(lambda e: e.copy(f8(nx), p2[0:64, :]), [p2n], [nxn])
            yield
            if lev < 5:
                b, p3, p3n = nb()
                for p in range(8):
                    mm(kb, p3[0:64, p * 64:(p + 1) * 64], cx[:, p, :], cxt[:, p, :], True, True, [cxn, cxtn], [p3n])
                act(lambda e: e.copy(f8(nxt_), p3[0:64, :]), [p3n], [nxtn])
            b, p4, p4n = nb()
            for p in range(8):
                mm(kb, p4[0:64, p * 64:(p + 1) * 64], nx[:, p, :], TT[:, p, :], True, True, [nxn, "TT"], [p4n])
            dve(lambda e: e.tensor_tensor(out=f8(TT), in0=f8(TT), in1=p4[0:64, :], op=ALU.add), ["TT", p4n], ["TT"])
            yield
            cx, cxn, cxt, cxtn, nx, nxn, nxt_, nxtn = nx, nxn, nxt_, nxtn, cx, cxn, cxt, cxtn
        yield
        b, pg, pgn = nb()
        for p in range(8):
            mm(kb, pg[0:64, p * 64:(p + 1) * 64], A_tm[:, p, :], TT[:, p, :], True, True, ["A_tm", "TT"], [pgn])
        act(lambda e: e.copy(f8(GT), pg[0:64, :]), [pgn], ["GT" + PAR])
        yield
        b, pw, pwn = nb()
        for p in range(8):
            mm(kb, pw[0:64, p * 64:(p + 1) * 64], AKT[:, p, :], V_tm[:, p, :], True, True, ["AKT", "V_tm" + PAR], [pwn])
        act(lambda e: e.copy(f8(Wb), pw[0:64, :]), [pwn], ["Wb"])
        yield
        b, ph_, phn = nb()
        for p in range(8):
            mm(kb, ph_[0:64, p * 64:(p + 1) * 64], TT[:, p, :], Wb[:, p, :], True, True, ["TT", "Wb"], [phn])
        act(lambda e: e.copy(f8(Hb), ph_[0:64, :]), [phn], ["Hb" + PAR])

    def stage2(tt):
        nonlocal si
        par = tt % 2
        PAR = str(par)
        Pt, RT = D2["Pt"][par], D2["RT"][par]
        BP_tm, KP_tm, V_tm, RBT, RKT, GT, Hb, rkb = (D2[k_][par] for k_ in ("BP_tm", "KP_tm", "V_tm", "RBT", "RKT", "GT", "Hb", "rkb"))
        for c2 in range(2):
            cur, curn, nxs, nxsn = ST[si], f"ST{si}", ST[1 - si], f"ST{1 - si}"
            curb, curbn, nxb, nxbn = STb[si], f"STb{si}", STb[1 - si], f"STb{1 - si}"
            b, pu, pun = nb()
            for h in range(4):
                p = c2 * 4 + h
                mm(kb, pu[0:64, h * 64:(h + 1) * 64], GT[:, p, :], curb[:, h, :], True, False, ["GT" + PAR, curbn], [pun])
                mm(kb, pu[0:64, h * 64:(h + 1) * 64], identb64, Hb[:, p, :], False, True, ["identb", "Hb" + PAR], [pun])
            act(lambda e: e.copy(Usb[:].rearrange("p a b -> p (a b)"), pu[0:64, 0:256]), [pun], ["Usb"])
            yield
            b, pS, pSn = nb()
            for h in range(4):
                p = c2 * 4 + h
                mm(kb, pS[0:64, h * 64:(h + 1) * 64], BP_tm[:, p, :], Usb[:, h, :], True, False, ["BP_tm" + PAR, "Usb"], [pSn])
                mm(kb, pS[0:64, h * 64:(h + 1) * 64], KP_tm[:, p, :], V_tm[:, p, :], False, True, ["KP_tm" + PAR, "V_tm" + PAR], [pSn])
            b, pY, pYn = nb()
            for h in range(4):
                p = c2 * 4 + h
                mm(kb, pY[0:64, h * 64:(h + 1) * 64], RT[:, h, c2 * 64:(c2 + 1) * 64], curb[:, h, :], True, False, ["RT" + PAR, curbn], [pYn])
                mm(kb, pY[0:64, h * 64:(h + 1) * 64], RBT[:, p, :], Usb[:, h, :], False, False, ["RBT" + PAR, "Usb"], [pYn])
                mm(kb, pY[0:64, h * 64:(h + 1) * 64], RKT[:, p, :], V_tm[:, p, :], False, True, ["RKT" + PAR, "V_tm" + PAR], [pYn])
            plb = Pt[:, :, c2 * 64 + 63:c2 * 64 + 64].to_broadcast([64, 4, 64])
            dve(lambda e: e.tensor_tensor(out=nxs[:], in0=cur[:], in1=plb, op=ALU.mult), [curn, "Pt" + PAR], [nxsn])
            dve(lambda e: e.tensor_tensor(out=nxs[:].rearrange("p a b -> p (a b)"), in0=nxs[:].rearrange("p a b -> p (a b)"), in1=pS[0:64, 0:256], op=ALU.add),
                [nxsn, pSn], [nxsn])
            act(lambda e: e.copy(nxb[:], nxs[:]), [nxsn], [nxbn])
            si = 1 - si
            yield
            act(lambda e: e.copy(Y[:].rearrange("p a b -> p (a b)"), pY[0:64, 0:256]), [pYn], ["Y"])
            dve(lambda e: e.reduce_sum(out=stt[:, 0:4], in_=Y[:], axis=AX.X), ["Y"], ["stt"])
            dve(lambda e: e.tensor_scalar(out=stt[:, 4:8], in0=stt[:, 0:4], scalar1=1.0 / 64, scalar2=None, op0=ALU.mult), ["stt"], ["stt"])
            dve(lambda e: e.tensor_tensor(out=Yc[:], in0=Y[:], in1=stt[:, 4:8].rearrange("p (h o) -> p h o", o=1).to_broadcast([64, 4, 64]), op=ALU.subtract),
                ["Y", "stt"], ["Yc"])
            act(lambda e: e.activation(out=Ysq[:].rearrange("p a b -> p (a b)"), in_=Yc[:].rearrange("p a b -> p (a b)"), func=AF.Square), ["Yc"], ["Ysq"])
            dve(lambda e: e.reduce_sum(out=stt[:, 8:12], in_=Ysq[:], axis=AX.X), ["Ysq"], ["stt2"])
            yield
            rsqrt_op(C, stt[:, 12:16], stt[:, 8:12], 1.0 / 64, 64e-5, ["stt2"], ["stt3"])
            dve(lambda e: e.tensor_tensor(out=Yc[:], in0=Yc[:], in1=stt[:, 12:16].rearrange("p (h o) -> p h o", o=1).to_broadcast([64, 4, 64]), op=ALU.mult),
                ["Yc", "stt3"], ["Yc"])
            ycf = Yc[:].rearrange("p a b -> p (a b)")
            dve(lambda e: e.tensor_tensor(out=ycf, in0=ycf, in1=gng, op=ALU.mult), ["Yc", "bcv"], ["Yc"])
            dve(lambda e: e.tensor_tensor(out=ycf, in0=ycf, in1=gnb, op=ALU.add), ["Yc", "bcv"], ["Yc"])
            dve(lambda e: e.tensor_tensor(out=Ysq[:], in0=V_tm[:, c2 * 4:c2 * 4 + 4, :], in1=rkb[:, c2 * 4:c2 * 4 + 4, 0:1].to_broadcast([64, 4, 64]), op=ALU.mult),
                ["V_tm" + PAR, "rkb" + PAR], ["Ysq"])
            dve(lambda e: e.tensor_tensor(out=Yc[:], in0=Yc[:], in1=Ysq[:], op=ALU.add), ["Yc", "Ysq"], ["Yc"])
            yield
            tok64 = slice(tt * 128 + c2 * 64, tt * 128 + (c2 + 1) * 64)
            b, pzd, pzdn = nb()
            for k in range(8):
                mm(kb, pzd[0:64, 0:256], C.hT[:, k, tok64], wZ[:, k, :], k == 0, k == 7, ["hT", "wZ"], [pzdn])
            act(lambda e: e.activation(out=szd[:], in_=pzd[0:64, 0:256], func=AF.Silu), [pzdn], ["szd"])
            yield
            dve(lambda e: e.tensor_tensor(out=md[:], in0=ycf, in1=szd[:], op=ALU.mult), ["Yc", "szd"], ["md"])
            if C.debug:
                kb.dma("sp", out=C.dbg_yd[tok64, :], in_=Y[:].rearrange("p a b -> p (a b)"), R=["Y"], W=["dbg_yd"])
            ms = (tt * 2 + c2) % 2
            pb = C.psb[ms]
            for j in range(2):
                kb.op("pe", lambda e: e.transpose(pb[:, j * 64:(j + 1) * 64], md[:, j * 128:(j + 1) * 128], C.identb[0:64, 0:64]),
                      R=["md", "identb"], W=[f"psb{ms}"])
            act(lambda e: e.copy(mtR[ms][:], pb[:, 0:128].rearrange("p (k t) -> p k t", k=2)), [f"psb{ms}"], [f"mtR{ms}"])
            kb.dma("sp", out=mixv[:, 6:8, tok64], in_=mtR[ms][:], R=[f"mtR{ms}"], W=["mixT_D"])
            yield

    si = 0

    def run_interleaved(ga, gb, ratio=2):
        a_live, b_live = ga is not None, gb is not None
        while a_live or b_live:
            for _ in range(ratio):
                if a_live:
                    try:
                        next(ga)
                    except StopIteration:
                        a_live = False
            if b_live:
                try:
                    next(gb)
                except StopIteration:
                    b_live = False

    run_interleaved(stage1(0), None)
    for tt in range(32):
        run_interleaved(stage1(tt + 1) if tt + 1 < 32 else None, stage2(tt))


def host_prep(inp):
    f = lambda k: np.asarray(inp[k], dtype=np.float32)
    bcs_, cols_, gpp_ = [], [], []
    bc_off = col_off = None
    for l in range(NL):
        b = Pack(128)
        b.add("sgu_ln_g", bc_rows(f("sgu_ln_g")[l]))
        b.add("sgu_ln_b", bc_rows(f("sgu_ln_b")[l]))
        b.add("gn_g", bc_rows(f("rwkv_gn_g")[l]))
        b.add("gn_b", bc_rows(f("rwkv_gn_b")[l]))
        c = Pack(128)
        c.add("sgu_bT", f("sgu_b")[l].T)
        cw = f("conv_w")[l]
        c.add("conv_wT", np.concatenate([cw[:, j * 128:(j + 1) * 128].T for j in range(2)], axis=1))
        for nm in ("conv_b", "conv_ln_g", "conv_ln_b", "conv_pw_b"):
            c.add(nm, col_chunks(f(nm)[l]))
        c.add("relb31", bc_rows(f("rel_bias")[31]))
        pad64 = lambda a: np.concatenate([a, np.zeros((128 - a.shape[0], a.shape[1]), np.float32)], axis=0)
        mu = f("rwkv_mu")[l]
        mu_cols = np.zeros((64, 14), np.float32)
        mu_cols[:, 0:12] = mu[0:768].reshape(12, 64).T
        mu_cols[0:32, 12] = mu[768:800]
        mu_cols[0:32, 13] = mu[800:832]
        c.add("rw_mu", pad64(mu_cols))
        for nm, key in (("rw_w0", "rwkv_w0"), ("rw_a0", "rwkv_a0"), ("rw_kk", "rwkv_k_k"), ("rw_ka", "rwkv_k_a")):
            c.add(nm, pad64(f(key)[l].reshape(4, 64).T))
        c.add("rw_rk", pad64(f("rwkv_r_k")[l].T))
        gpp_.append(np.concatenate([bc_rows(f("g_pre")[l]), bc_rows(f("g_post")[l])], axis=1))
        bcs_.append(b.build()); cols_.append(c.build())
        bc_off, col_off = b.off, c.off
    cst = np.zeros((128, 1024), np.float32)
    cst[:, 0:128] = np.eye(128, dtype=np.float32)
    ii = np.arange(128)
    cst[:, 128:256] = (ii[None, :] >= ii[:, None]).astype(np.float32)
    cst[:, 256:384] = 1.0 / 256.0
    rb = f("rel_bias")
    tpos = np.arange(S)
    npos = np.arange(256)
    dc = tpos[None, :] - (16 * npos[:, None] + 31)
    cb = np.where((dc >= 0)[None], rb[rel_bucket_np(dc)].transpose(2, 0, 1), np.float32(NEGM)).astype(np.float32)
    cb[:, 255, :] = NEGM
    def strip(width, lim):
        dd = np.arange(width)[None, :] - 384 - np.arange(128)[:, None]
        ok = (dd >= 0) if lim is None else ((dd >= 0) & (dd < lim))
        v = rb[rel_bucket_np(dd)]
        return np.ascontiguousarray(np.where(ok[:, :, None], v, np.float32(NEGM)).transpose(0, 2, 1).astype(np.float32))
    gs_, gw_ = strip(1024, None), strip(1408, 512)
    xbig = np.zeros((64, S), np.float32)
    xbig[np.arange(S) // 64, np.arange(S)] = 1.0
    mmat = np.zeros((256, 64), np.float32)
    for j in range(64):
        for n in range(4 * j - 1, 4 * j + 4):
            if 0 <= n < 255:
                mmat[n, j] = 1.0
    tk_mul = np.zeros((32, 128, 64), np.float32)
    tk_add = np.zeros((32, 128, 64), np.float32)
    tt_ = np.arange(S).reshape(32, 128)
    cur = tt_ // 64
    jj = np.arange(64)[None, None, :]
    causal = jj <= cur[:, :, None]
    f0 = (jj == 0) & causal
    f1 = (jj == cur[:, :, None])
    f2 = (jj == cur[:, :, None] - 1)
    forced = f0 | f1 | f2
    tk_mul[causal & ~forced] = 1.0
    tk_add[~causal] = -1e30
    tk_add[f0] = 10000.0
    tk_add[f1] = 10001.0
    tk_add[f2] = 10002.0
    i64 = np.arange(64)
    mus = (i64[None, :] > i64[:, None]).astype(np.float32)
    mui = (i64[None, :] >= i64[:, None]).astype(np.float32)
    mls = (i64[None, :] < i64[:, None]).astype(np.float32)
    rmasks = np.stack([np.tile(m_, (1, 8)) for m_ in (mus, mui, mls, np.eye(64, dtype=np.float32))], axis=1)
    shared = {
        "rwkv_w_up": f("rwkv_w_up"), "rwkv_a_up": f("rwkv_a_up"), "rmasks": np.ascontiguousarray(rmasks),
        "nsa_w1": f("nsa_w1"), "nsa_w2": f("nsa_w2"),
        "nsa_posT": np.ascontiguousarray(f("nsa_pos").transpose(0, 1, 3, 2)),
        "xbig": xbig, "gs": gs_, "gw": gw_, "mmat": mmat, "tk_mul": tk_mul, "tk_add": tk_add, "cbias": cb,
        "w_in": f("w_in"), "w_out": f("w_out"), "ple_gate": f("ple_gate"), "ple_proj": f("ple_proj"),
        "bc": np.stack(bcs_), "col": np.stack(cols_), "cst": cst, "gpp": np.ascontiguousarray(np.stack(gpp_)),
        "sgu_wT": np.ascontiguousarray(f("sgu_w").transpose(0, 3, 1, 2)),
        "conv_pw": f("conv_pw"),
    }
    return shared, bc_off, col_off


_CACHE = {}
LAUNCH_GROUPS = ((0, 1),)


def kernel(**inputs):
    shared, bc_off, col_off = host_prep(inputs)
    n_bc, n_col = shared["bc"].shape[2], shared["col"].shape[2]
    x = np.asarray(inputs["x"], dtype=np.float32)
    p = np.asarray(inputs["p"], dtype=np.float32)
    cur = [np.ascontiguousarray(x[b]) for b in range(4)]
    for grp in LAUNCH_GROUPS:
        key = (n_bc, n_col, grp)
        if key not in _CACHE:
            _CACHE[key] = build_program(bc_off, col_off, n_bc, n_col, layers=grp)
        nc, kb = _CACHE[key]
        in_maps = []
        for c in range(8):
            b = c % 4
            m = dict(shared)
            m["x"] = cur[b]
            m["p"] = np.ascontiguousarray(p[:, b])
            in_maps.append(m)
        res = run_bass_kernel_spmd(nc, in_maps, core_ids=list(range(8)))
        cur = [np.ascontiguousarray(np.asarray(res.results[b]["out"], dtype=np.float32)) for b in range(4)]
    return np.stack(cur, axis=0)
```
